# Optimizing a Trainium2 kernel written in Bass

```python
import math
import jax, jax.numpy as jnp
from jax import lax
import numpy as np

D_MODEL = 1024
BATCH = 2
SEQ = 8192
DEPTH = 2
DEC_BATCH = 128
DEC_SEQ = 1
PAST_LEN = 8192
PAGE_SIZE = 128

N_HEADS = 16
N_KV_HEADS = 4
GROUP = N_HEADS // N_KV_HEADS
HEAD_DIM = D_MODEL // N_HEADS
Q_W = N_HEADS * HEAD_DIM
KV_W = N_KV_HEADS * HEAD_DIM
WINDOW = 128
BLOCK = 128
N_BUCKETS = 32
MAX_DISTANCE = 128
D_CONV = D_MODEL
CONV_W = 3
D_FF = 2816
RMS_EPS = 1e-6
NEG = -1e30
W_BUF = min(WINDOW, PAST_LEN)
IN_W = 3 * D_CONV + Q_W + 2 * KV_W + 2 * D_MODEL
IN_SPLITS = (D_CONV, 2 * D_CONV, 3 * D_CONV, 3 * D_CONV + Q_W, 3 * D_CONV + Q_W + KV_W, 3 * D_CONV + Q_W + 2 * KV_W, 3 * D_CONV + Q_W + 2 * KV_W + D_MODEL)

kernel_name = "hybrid_conv_swa_sink_macaron_step"


def rmsnorm(x, g):
    xf = x.astype(jnp.float32)
    r = lax.rsqrt(jnp.mean(xf * xf, axis=-1, keepdims=True) + RMS_EPS)
    return (xf * r).astype(x.dtype) * g


def swiglu(h, w_gu, w_down):
    gate, up = jnp.split(h @ w_gu, 2, axis=-1)
    return (jax.nn.silu(gate) * up) @ w_down


def t5_bucket(rel):
    n = jnp.maximum(rel, 0)
    max_exact = N_BUCKETS // 2
    nf = jnp.maximum(n, 1).astype(jnp.float32)
    large = max_exact + (jnp.log(nf / max_exact) / math.log(MAX_DISTANCE / max_exact) * (N_BUCKETS - max_exact)).astype(jnp.int32)
    large = jnp.minimum(large, N_BUCKETS - 1)
    return jnp.where(n < max_exact, n, large)


def rel_bias_logits(rel, rel_bias):
    b = rel_bias.astype(jnp.float32)[t5_bucket(rel)]
    return jnp.moveaxis(b, -1, 0).reshape(N_KV_HEADS, GROUP, *rel.shape)


def sink_attend(q, k, v, bias, mask, sinks):
    s = jnp.einsum('...qhgd,...khd->...hgqk', q.astype(jnp.float32), k.astype(jnp.float32)) * (HEAD_DIM ** -0.5) + bias
    s = jnp.where(mask, s, NEG)
    sink = sinks.astype(jnp.float32).reshape(N_KV_HEADS, GROUP, 1, 1)
    m = jnp.maximum(jnp.max(s, axis=-1, keepdims=True), sink)
    p = jnp.exp(s - m)
    denom = jnp.sum(p, axis=-1, keepdims=True) + jnp.exp(sink - m)
    o = jnp.einsum('...hgqk,...khd->...qhgd', p / denom, v.astype(jnp.float32))
    return o.astype(v.dtype)


def banded_window_attention(q, k, v, sinks, rel_bias):
    b_, s_ = q.shape[0], q.shape[1]
    nb = s_ // BLOCK
    qb = q.reshape(b_, nb, BLOCK, N_KV_HEADS, GROUP, HEAD_DIM)

    def band(t):
        tb = t.reshape(b_, nb, BLOCK, N_KV_HEADS, HEAD_DIM)
        prev = jnp.concatenate([jnp.zeros_like(tb[:, :1]), tb[:, :-1]], axis=1)
        return jnp.concatenate([prev, tb], axis=2)

    qi = jnp.arange(BLOCK, dtype=jnp.int32)
    kj = jnp.arange(2 * BLOCK, dtype=jnp.int32)
    rel = qi[:, None] + BLOCK - kj[None, :]
    bias = rel_bias_logits(rel, rel_bias)
    key_abs = jnp.arange(nb, dtype=jnp.int32)[:, None] * BLOCK - BLOCK + kj[None, :]
    mask = ((rel >= 0) & (rel < WINDOW))[None] & (key_abs >= 0)[:, None, :]
    o = sink_attend(qb, band(k), band(v), bias, mask[:, None, None], sinks)
    w = min(WINDOW, s_)
    return o.reshape(b_, s_, N_KV_HEADS, GROUP, HEAD_DIM), k[:, -w:], v[:, -w:]


def decode_window_attention(q, k, v, k_buf, v_buf, sinks, rel_bias):
    t_ = q.shape[1]
    wb = k_buf.shape[1]
    kc = jnp.concatenate([k_buf.astype(k.dtype), k], axis=1)
    vc = jnp.concatenate([v_buf.astype(v.dtype), v], axis=1)
    qpos = PAST_LEN + jnp.arange(t_, dtype=jnp.int32)
    kpos = jnp.concatenate([PAST_LEN - wb + jnp.arange(wb, dtype=jnp.int32), qpos])
    rel = qpos[:, None] - kpos[None, :]
    bias = rel_bias_logits(rel, rel_bias)
    mask = (rel >= 0) & (rel < WINDOW)
    o = sink_attend(q, kc, vc, bias, mask, sinks)
    return o, kc[:, -wb:], vc[:, -wb:]


def token_mixer(h, conv_state, k_buf, v_buf, w_in, conv_w, w_conv_out, w_attn_out, w_out, sinks, rel_bias):
    prompt = conv_state is None
    nb_, t_ = h.shape[0], h.shape[1]
    cb, cc, cx, q, k, v, ga, gb = jnp.split(h @ w_in, IN_SPLITS, axis=-1)
    u = cc * cx
    if prompt:
        upad = jnp.concatenate([jnp.zeros((nb_, CONV_W - 1, D_CONV), u.dtype), u], axis=1)
    else:
        upad = jnp.concatenate([conv_state.astype(u.dtype), u], axis=1)
    yc = conv_w[0] * upad[:, 0:t_]
    for i in range(1, CONV_W):
        yc = yc + conv_w[i] * upad[:, i:i + t_]
    a_out = (cb * yc) @ w_conv_out
    new_conv = upad[:, -(CONV_W - 1):]
    q = q.reshape(nb_, t_, N_KV_HEADS, GROUP, HEAD_DIM)
    k = k.reshape(nb_, t_, N_KV_HEADS, HEAD_DIM)
    v = v.reshape(nb_, t_, N_KV_HEADS, HEAD_DIM)
    if prompt:
        o, new_k, new_v = banded_window_attention(q, k, v, sinks, rel_bias)
    else:
        o, new_k, new_v = decode_window_attention(q, k, v, k_buf, v_buf, sinks, rel_bias)
    att = o.reshape(nb_, t_, Q_W) @ w_attn_out
    merged = jax.nn.sigmoid(ga) * a_out + jax.nn.sigmoid(gb) * att
    return merged @ w_out, new_conv, new_k, new_v


def decoder_layer(x, conv_state, k_buf, v_buf, g, w_ff1_gu, w_ff1_down, w_in, conv_w, sinks, w_conv_out, w_attn_out, w_out, w_ff2_gu, w_ff2_down, rel_bias):
    x = x + 0.5 * rmsnorm(swiglu(rmsnorm(x, g[0]), w_ff1_gu, w_ff1_down), g[1])
    m, new_conv, new_k, new_v = token_mixer(rmsnorm(x, g[2]), conv_state, k_buf, v_buf, w_in, conv_w, w_conv_out, w_attn_out, w_out, sinks, rel_bias)
    x = x + rmsnorm(m, g[3])
    x = x + 0.5 * rmsnorm(swiglu(rmsnorm(x, g[4]), w_ff2_gu, w_ff2_down), g[5])
    return x, new_conv, new_k, new_v


def setup_inputs(seed: int = 0) -> dict:
    key = jax.random.key(seed)
    ks = jax.random.split(key, 20)
    f32 = jnp.float32
    nrm = lambda k, shape, scale: jax.random.normal(k, shape, f32) * scale
    return {
        "x_prompt": nrm(ks[0], (BATCH, SEQ, D_MODEL), 1.0),
        "x_sample": nrm(ks[1], (DEC_BATCH, DEC_SEQ, D_MODEL), 1.0),
        "state_conv": nrm(ks[2], (DEPTH, DEC_BATCH, CONV_W - 1, D_CONV), 1.0),
        "cache_k_win": nrm(ks[3], (DEPTH, DEC_BATCH, W_BUF, N_KV_HEADS, HEAD_DIM), 1.0),
        "cache_v_win": nrm(ks[4], (DEPTH, DEC_BATCH, W_BUF, N_KV_HEADS, HEAD_DIM), 1.0),
        "rel_bias": nrm(ks[5], (N_BUCKETS, N_HEADS), 0.5),
        "norm_g": 1.0 + nrm(ks[6], (DEPTH, 6, D_MODEL), 0.05),
        "w_ff1_gu": nrm(ks[7], (DEPTH, D_MODEL, 2 * D_FF), D_MODEL ** -0.5),
        "w_ff1_down": nrm(ks[8], (DEPTH, D_FF, D_MODEL), D_FF ** -0.5),
        "w_in": nrm(ks[9], (DEPTH, D_MODEL, IN_W), D_MODEL ** -0.5),
        "conv_w": nrm(ks[10], (DEPTH, CONV_W, D_CONV), CONV_W ** -0.5),
        "sinks": nrm(ks[11], (DEPTH, N_HEADS), 0.5),
        "w_conv_out": nrm(ks[12], (DEPTH, D_CONV, D_MODEL), D_CONV ** -0.5),
        "w_attn_out": nrm(ks[13], (DEPTH, Q_W, D_MODEL), Q_W ** -0.5),
        "w_out": nrm(ks[14], (DEPTH, D_MODEL, D_MODEL), D_MODEL ** -0.5),
        "w_ff2_gu": nrm(ks[15], (DEPTH, D_MODEL, 2 * D_FF), D_MODEL ** -0.5),
        "w_ff2_down": nrm(ks[16], (DEPTH, D_FF, D_MODEL), D_FF ** -0.5),
    }


def reference(x_prompt, x_sample, state_conv, cache_k_win, cache_v_win, rel_bias, norm_g, w_ff1_gu, w_ff1_down, w_in, conv_w, sinks, w_conv_out, w_attn_out, w_out, w_ff2_gu, w_ff2_down):
    yp, ys = x_prompt, x_sample
    pc, pk, pv, sc, sk, sv = [], [], [], [], [], []
    for l in range(DEPTH):
        yp, c1, k1, v1 = decoder_layer(yp, None, None, None, norm_g[l], w_ff1_gu[l], w_ff1_down[l], w_in[l], conv_w[l], sinks[l], w_conv_out[l], w_attn_out[l], w_out[l], w_ff2_gu[l], w_ff2_down[l], rel_bias)
        ys, c2, k2, v2 = decoder_layer(ys, state_conv[l], cache_k_win[l], cache_v_win[l], norm_g[l], w_ff1_gu[l], w_ff1_down[l], w_in[l], conv_w[l], sinks[l], w_conv_out[l], w_attn_out[l], w_out[l], w_ff2_gu[l], w_ff2_down[l], rel_bias)
        pc.append(c1); pk.append(k1); pv.append(v1)
        sc.append(c2); sk.append(k2); sv.append(v2)
    return (yp, ys, jnp.stack(pc), jnp.stack(pk), jnp.stack(pv), jnp.stack(sc), jnp.stack(sk), jnp.stack(sv))
```

```python
import math
from contextlib import ExitStack
import numpy as np
import concourse.bass as bass
import concourse.mybir as mybir
from concourse.bass_utils import run_bass_kernel_spmd

F32 = mybir.dt.float32
BF16 = mybir.dt.bfloat16
AF = mybir.ActivationFunctionType
ALU = mybir.AluOpType
AX = mybir.AxisListType

D = 1024
NH = 16
NKV = 4
HD = 64
N_BUCKETS = 32
MAX_DISTANCE = 128
IN_W = 6656
EPS = 1e-6
NEGBIG = -1.0e9


class Sem:
    def __init__(self, handle):
        self.h = handle
        self.n = 0


class Eng:
    def __init__(self, name, handle, sem, is_pe=False):
        self.name = name
        self.h = handle
        self.sem = sem
        self.known = {}
        self.is_pe = is_pe


class Rec:
    __slots__ = ("w", "r")

    def __init__(self):
        self.w = {}
        self.r = {}


class T:
    def __init__(self, name, ap):
        self.name = name
        self.ap = ap
        self.recs = {None: Rec()}

    def __getitem__(self, k):
        return self.ap[k]


def _acc(x):
    return x if isinstance(x, tuple) else (x, None)


def _put(d, sv):
    s, v = sv
    k = id(s)
    if k not in d or d[k][1] < v:
        d[k] = (s, v)


class FW:
    def __init__(self, nc, es):
        self.nc = nc
        self.es = es
        self.nsem = 0
        self.dma_sems = []
        self.pe = Eng("pe", nc.tensor, self.new_sem(False), is_pe=True)
        self.act = Eng("act", nc.scalar, self.new_sem(False))
        self.dve = Eng("dve", nc.vector, self.new_sem(False))
        self.pool = Eng("pool", nc.gpsimd, self.new_sem(False))
        self.sp = Eng("sp", nc.sync, self.new_sem(False))
        self.ninstr = 0

    def new_sem(self, dma=True):
        self.nsem += 1
        h = self.es.enter_context(self.nc.semaphore("s%d" % self.nsem))
        s = Sem(h)
        if dma:
            self.dma_sems.append(s)
        return s

    def sbuf(self, name, shape, dtype):
        return T(name, self.es.enter_context(self.nc.sbuf_tensor("sb_" + name, shape, dtype)))

    def psum(self, name, shape, dtype):
        return T(name, self.es.enter_context(self.nc.psum_tensor("ps_" + name, shape, dtype)))

    def _recs(self, t, key):
        if key is None:
            return list(t.recs.values())
        out = [t.recs[None]]
        if key in t.recs:
            out.append(t.recs[key])
        return out

    def _needs(self, reads, writes):
        needs = {}
        for a in reads:
            t, key = _acc(a)
            for rec in self._recs(t, key):
                for sv in rec.w.values():
                    _put(needs, sv)
        for a in writes:
            t, key = _acc(a)
            for rec in self._recs(t, key):
                for sv in rec.w.values():
                    _put(needs, sv)
                for sv in rec.r.values():
                    _put(needs, sv)
        return needs

    def _wait(self, eng, needs):
        for k, (s, v) in needs.items():
            if s is eng.sem and eng.is_pe:
                continue
            if eng.known.get(k, 0) >= v:
                continue
            eng.h.wait_ge(s.h, v)
            eng.known[k] = v
            self.ninstr += 1

    def _mark(self, sv, reads, writes):
        wset = set()
        for a in writes:
            t, key = _acc(a)
            wset.add((id(t), key))
            if key is None:
                rec = Rec()
                t.recs = {None: rec}
            else:
                rec = t.recs.get(key)
                if rec is None:
                    rec = t.recs[key] = Rec()
                rec.w = {}
                rec.r = {}
            _put(rec.w, sv)
        for a in reads:
            t, key = _acc(a)
            if (id(t), key) in wset:
                continue
            if key is None:
                for rec in t.recs.values():
                    _put(rec.r, sv)
            else:
                rec = t.recs.get(key)
                if rec is None:
                    rec = t.recs[key] = Rec()
                _put(rec.r, sv)

    def op(self, eng, fn, reads=(), writes=(), inc=True):
        self._wait(eng, self._needs(reads, writes))
        ins = fn()
        self.ninstr += 1
        if inc:
            eng.sem.n += 1
            ins.then_inc(eng.sem.h, 1)
            v = eng.sem.n
        else:
            v = eng.sem.n + 1
        self._mark((eng.sem, v), reads, writes)
        return ins

    def dma(self, q, out_ap, in_ap, sem, reads=(), writes=(), **kw):
        self._wait(q, self._needs(reads, writes))
        ins = q.h.dma_start(out=out_ap, in_=in_ap, **kw)
        sem.n += 16
        ins.then_inc(sem.h, 16)
        self.ninstr += 1
        self._mark((sem, sem.n), reads, writes)
        return ins

    def inherit(self, dst, srcs):
        rec = Rec()
        for s in srcs:
            for r in s.recs.values():
                for sv in r.w.values():
                    _put(rec.r, sv)
                for sv in r.r.values():
                    _put(rec.r, sv)
        dst.recs = {None: rec}

    def wait_all(self, eng, sems):
        for s in sems:
            if s.n > 0 and eng.known.get(id(s), 0) < s.n:
                eng.h.wait_ge(s.h, s.n)
                eng.known[id(s)] = s.n


def slot_head(s):
    c, hh = divmod(s, 2)
    g = 2 * (c // 4) + hh
    return 4 * g + (c % 4)


def t5_bucket_np(rel):
    n = np.maximum(rel, 0)
    max_exact = N_BUCKETS // 2
    nf = np.maximum(n, 1).astype(np.float32)
    large = max_exact + (np.log(nf / np.float32(max_exact)) / np.float32(math.log(MAX_DISTANCE / max_exact))
                         * np.float32(N_BUCKETS - max_exact)).astype(np.int32)
    large = np.minimum(large, N_BUCKETS - 1)
    return np.where(n < max_exact, n, large)


class Cfg:
    def __init__(self, NG=3, NB=6, DFF=2816, L=2, NS=16, R=3):
        self.NG, self.NB, self.DFF, self.L, self.NS, self.R = NG, NB, DFF, L, NS, R
        self.NBT = NG * NB
        self.NV = self.NBT - 2
        self.G = NB * 128
        self.GX = self.G + NS
        self.NFC = DFF // 128
        self.SPAN = self.NBT * 128


class _Stop(Exception):
    pass


def build(cfg, stop_after=None):
    NG, NB, DFF, L, NS, R = cfg.NG, cfg.NB, cfg.DFF, cfg.L, cfg.NS, cfg.R
    G, GX, NFC, SPAN, NV = cfg.G, cfg.GX, cfg.NFC, cfg.SPAN, cfg.NV
    nc = bass.Bass("TRN2", target_bir_lowering=False)

    def din(name, shape):
        return nc.dram_tensor(name, list(shape), F32, kind="ExternalInput").ap()

    def dout(name, shape):
        return nc.dram_tensor(name, list(shape), F32, kind="ExternalOutput").ap()

    xT_d = din("xT", [D, SPAN])
    xsT_d = din("xsT", [D, NS])
    flag_d = din("flag", [128, 1])
    stT_d = din("stT", [L, 128, 8, 2, NS])
    kcT_d = din("kcT", [L, 128, 2, NS, 128])
    kc_d = din("kc", [L, NS, 128, 256])
    vc_d = din("vc", [L, NS, 128, 256])
    relb_d = din("relb", [32, 16])
    relbrep_d = din("relbrep", [32, 128])
    oh_d = din("oh", [32, 384])
    ohdec_d = din("ohdec", [32, 128])
    negm_d = din("negm", [1, 384])
    negmdec_d = din("negmdec", [1, 128])
    gT_d = din("gT", [128, L * 6 * 8])
    cwT_d = din("cwT", [128, L * 3 * 8])
    sinkb_d = din("sinkb", [128, L * 16])
    sinkd_d = din("sinkd", [128, L])
    w1gu_d = din("w1gu", [L, D, 2 * DFF])
    w1d_d = din("w1d", [L, DFF, D])
    win_d = din("win", [L, D, IN_W])
    wco_d = din("wco", [L, D, D])
    wao_d = din("wao", [L, D, D])
    wo_d = din("wo", [L, D, D])
    w2gu_d = din("w2gu", [L, D, 2 * DFF])
    w2d_d = din("w2d", [L, DFF, D])

    yT_o = dout("yT", [D, NV * 128])
    ysT_o = dout("ysT", [D, NS])
    convp_o = dout("convp", [L, 128, 8, 2])
    kpT_o = dout("kpT", [L, 128, 2, 128])
    vp_o = dout("vp", [L, 128, 256])
    convs_o = dout("convs", [L, 128, 8, 2, NS])
    ks_o = dout("ks", [L, NS, 128, 256])
    vs_o = dout("vs", [L, NS, 128, 256])
    scr = nc.dram_tensor("scr", [16 * 128 * 385 + 1024], F32, kind="Internal")

    with ExitStack() as es:
        fw = FW(nc, es)
        pe, act, dve, pool, sp = fw.pe, fw.act, fw.dve, fw.pool, fw.sp
        V_, A_, P_ = nc.vector, nc.scalar, nc.tensor

        xT = fw.sbuf("xT", [128, 8, GX], F32)
        hT = fw.sbuf("hT", [128, 8, GX], BF16)
        bfB = fw.sbuf("bfB", [128, 8, GX], BF16)
        rstd = fw.sbuf("rstd", [128, GX], F32)
        NKC = (4096 + GX - 1) // GX
        actT = fw.sbuf("actT", [128, max(NFC, 16 + NKC), GX], BF16)
        yT = fw.sbuf("yT", [128, 8, 2 + GX], F32)
        bfA = fw.sbuf("bfA", [128, 8, GX], BF16)
        kTz = [fw.sbuf("kTz%d" % i, [128, 2, 128 + GX], BF16) for i in range(2)]
        qsz = [fw.sbuf("qsz%d" % i, [128, 8, NS], BF16) for i in range(2)]
        Vt1 = fw.sbuf("Vt", [128, NB + 1, 256], BF16)
        Vl = [Vt1] * L
        kcar = [fw.sbuf("kcar%d" % l, [128, 2, 128], BF16) for l in range(L)]
        vcar = [fw.sbuf("vcar%d" % l, [128, 256], BF16) for l in range(L)]
        ucar = [fw.sbuf("ucar%d" % l, [128, 8, 2], F32) for l in range(L)]
        Bhi = fw.sbuf("Bhi", [128, 16, 256], BF16)
        Blo = fw.sbuf("Blo", [128, 16, 256], BF16)
        diag4 = [fw.sbuf("diag4_%d" % i, [128, 4, 128], BF16) for i in range(2)]
        mrow = fw.sbuf("mrow", [1, 256], BF16)
        ring = [fw.sbuf("ring%d" % i, [128, 4096], BF16) for i in range(R)]
        ring_sem = [fw.new_sem() for _ in range(R)]
        TMPW = 400 if G > 512 else 512
        tmpf = [fw.sbuf("tmpf%d" % i, [128, TMPW], F32) for i in range(3)]
        sbt = [fw.sbuf("sbt0", [128, 4, 256], F32)]
        attb = fw.sbuf("attb", [128, 4, 1024], BF16)
        Pn = [T("Pn%d" % i, attb.ap[:, i, :].rearrange("p (h k) -> p h k", h=4)) for i in range(2)]
        PTs = [T("PTs%d" % i, attb.ap[:, 2 + i, :]) for i in range(2)]
        st = [fw.sbuf("st%d" % i, [128, 32], F32) for i in range(2)]
        ident = fw.sbuf("ident", [128, 128], BF16)
        identf = fw.sbuf("identf", [128, 128], F32)
        ones = fw.sbuf("ones", [128, 128], BF16)
        gT = fw.sbuf("gT", [128, L * 48], F32)
        gH = fw.sbuf("gH", [128, L * 48], F32)
        cwT = fw.sbuf("cwT", [128, L * 24], F32)
        sinkb = fw.sbuf("sinkb", [128, L * 16], F32)
        nsinkb = fw.sbuf("nsinkb", [128, L * 16], F32)
        sinkd = fw.sbuf("sinkd", [128, L], F32)
        nsinkd = fw.sbuf("nsinkd", [128, L], F32)
        flag = fw.sbuf("flag", [128, 1], F32)
        maskadd = fw.sbuf("maskadd", [128, 1], F32)
        epst = fw.sbuf("epst", [128, 1], F32)
        KcT = T("KcT", actT.ap[:, 16:16 + NKC, :].rearrange("p a b -> p (a b)")[:, 0:2 * NS * 128].rearrange("p (g b j) -> p g b j", g=2, b=NS))
        Vc = T("Vc", attb.ap[:, :, :].rearrange("p a b -> p (a b)")[:, 0:NS * 256].rearrange("p (b f) -> p b f", b=NS))
        stT = fw.sbuf("stT", [128, 8, 2, NS], F32)
        Bd = fw.sbuf("Bd", [128, 128], F32)
        es2 = ExitStack()
        setupA = T("setupA", es2.enter_context(nc.sbuf_tensor("sb_setupA", [32, 16 + 128 + 384 + 128], F32)))
        setupB = T("setupB", es2.enter_context(nc.sbuf_tensor("sb_setupB", [1, 384 + 128 + 16 + 128], F32)))
        Tt = T("Tt", es2.enter_context(nc.sbuf_tensor("sb_Tt", [16, 384], F32)))

        PSA = fw.psum("PSA", [128, 1024], F32)
        PSB = fw.psum("PSB", [128, 1024], F32)
        P4 = fw.psum("P4", [128, 512], F32)
        P5 = fw.psum("P5", [128, 512], F32)
        P6 = fw.psum("P6", [128, 512], F32)
        PTB = fw.psum("PTB", [128, 512], F32)
        PSX = [PSA, PSB]
        POX = [P4, P5]

        scrT = T("scr", None)
        ksT = T("ks", None)
        vsT = T("vs", None)

        sem_x = fw.new_sem()
        sem_c = fw.new_sem()
        sem_scr = fw.new_sem()
        sem_y = fw.new_sem()
        sem_o = [fw.new_sem() for _ in range(8)]
        sem_kc = fw.new_sem()
        sem_vc = fw.new_sem()
        sem_st = fw.new_sem()
        sem_v0 = fw.new_sem()
        sem_copy = fw.new_sem()

        plan = []

        def plan_ffn(wgu, wd, l):
            v = wgu[l].rearrange("(c p) n -> p c n", p=128)
            for u in range(NFC // 2):
                plan.append(([(v[:, :, u * 256:(u + 1) * 256], 0, 256), (v[:, :, DFF + u * 256:DFF + (u + 1) * 256], 256, 512)], [8, 512]))
            v2 = wd[l].rearrange("(c p) n -> p c n", p=128)
            for oc in range(8):
                plan.append(([(v2[:, :, oc * 128:(oc + 1) * 128], 0, 128)], [NFC, 128]))

        IN_UNITS = [0, 1, 6, 7, 8, 9, 10, 11, 12]

        def plan_mixer(l):
            v = win_d[l].rearrange("(c p) n -> p c n", p=128)
            for u in range(4):
                plan.append(([(v[:, :, 1024 + u * 256:1024 + (u + 1) * 256], 0, 256), (v[:, :, 2048 + u * 256:2048 + (u + 1) * 256], 256, 512)], [8, 512]))
            for u in IN_UNITS:
                plan.append(([(v[:, :, u * 512:(u + 1) * 512], 0, 512)], [8, 512]))
            for w in (wco_d, wao_d, wo_d):
                vv = w[l].rearrange("(c p) n -> p c n", p=128)
                for hf in range(2):
                    plan.append(([(vv[:, :, hf * 512:(hf + 1) * 512], 0, 512)], [8, 512]))

        for gi in range(NG):
            for l in range(L):
                plan_ffn(w1gu_d, w1d_d, l)
                plan_mixer(l)
                plan_ffn(w2gu_d, w2d_d, l)

        wstate = {"issued": 0, "used": 0}

        def wissue():
            i = wstate["issued"]
            srcs, vs = plan[i]
            slot = i % R
            ne = vs[0] * vs[1]
            dst = ring[slot][:, 0:ne].rearrange("p (a b) -> p a b", a=vs[0])
            for k_, (src, lo, hi) in enumerate(srcs):
                fw.dma(pool, dst[:, :, lo:hi], src, ring_sem[slot], writes=[(ring[slot], k_)] if len(srcs) > 1 else [ring[slot]])
            if len(srcs) > 1:
                r_ = Rec()
                _put(r_.w, (ring_sem[slot], ring_sem[slot].n))
                ring[slot].recs = {None: r_}
            wstate["issued"] = i + 1

        def wnext():
            i = wstate["used"]
            while wstate["issued"] < min(i + R, len(plan)):
                wissue()
            assert wstate["issued"] > i
            src, vs = plan[i]
            slot = i % R
            ne = vs[0] * vs[1]
            wstate["used"] = i + 1
            return ring[slot], ring[slot][:, 0:ne].rearrange("p (a b) -> p a b", a=vs[0])

        fw.dma(sp, gT[:], gT_d, sem_c, writes=[gT])
        fw.dma(sp, cwT[:], cwT_d, sem_c, writes=[cwT])
        fw.dma(sp, sinkb[:], sinkb_d, sem_c, writes=[sinkb])
        fw.dma(sp, sinkd[:], sinkd_d, sem_c, writes=[sinkd])
        fw.dma(sp, flag[:], flag_d, sem_c, writes=[flag])
        fw.dma(sp, setupA[:, 0:16], relb_d, sem_c, writes=[setupA])
        fw.dma(sp, setupA[:, 16:144], relbrep_d, sem_c, writes=[setupA])
        fw.dma(sp, setupA[:, 144:528], oh_d, sem_c, writes=[setupA])
        fw.dma(sp, setupA[:, 528:656], ohdec_d, sem_c, writes=[setupA])
        fw.dma(sp, setupB[:, 0:384], negm_d, sem_c, writes=[setupB])
        fw.dma(sp, setupB[:, 384:512], negmdec_d, sem_c, writes=[setupB])
        for t_ in (gT, cwT, sinkb, sinkd, flag, setupA, setupB):
            r_ = Rec()
            _put(r_.w, (sem_c, sem_c.n))
            t_.recs = {None: r_}
        for l in range(L):
            fw.dma(sp, ks_o[l, :, 0:127, :], kc_d[l, :, 1:128, :], sem_copy, writes=[(ksT, ("c", l))])
            fw.dma(sp, vs_o[l, :, 0:127, :], vc_d[l, :, 1:128, :], sem_copy, writes=[(vsT, ("c", l))])

        fw.op(dve, lambda: V_.memset(ident[:], 0.0), writes=[ident])
        fw.op(pool, lambda: nc.gpsimd.affine_select(out=ident[:], in_=ident[:], compare_op=ALU.not_equal, fill=1.0,
                                                    base=0, pattern=[[-1, 128]], channel_multiplier=1),
              reads=[ident], writes=[ident])
        fw.op(dve, lambda: V_.tensor_copy(out=identf[:], in_=ident[:]), reads=[ident], writes=[identf])
        fw.op(dve, lambda: V_.memset(ones[:], 1.0), writes=[ones])
        fw.op(dve, lambda: V_.memset(epst[:], EPS), writes=[epst])
        fw.op(dve, lambda: V_.memset(setupB[:, 512:656], 1.0), writes=[setupB])
        fw.op(dve, lambda: V_.tensor_scalar(out=gH[:], in0=gT[:], scalar1=0.5, scalar2=None, op0=ALU.mult),
              reads=[gT], writes=[gH])
        fw.op(dve, lambda: V_.tensor_scalar(out=nsinkb[:], in0=sinkb[:], scalar1=-1.0, scalar2=None, op0=ALU.mult),
              reads=[sinkb], writes=[nsinkb])
        fw.op(dve, lambda: V_.tensor_scalar(out=nsinkd[:], in0=sinkd[:], scalar1=-1.0, scalar2=None, op0=ALU.mult),
              reads=[sinkd], writes=[nsinkd])
        fw.op(dve, lambda: V_.tensor_scalar(out=maskadd[:], in0=flag[:], scalar1=-NEGBIG, scalar2=NEGBIG,
                                            op0=ALU.mult, op1=ALU.add), reads=[flag], writes=[maskadd])
        for l in range(L):
            fw.op(dve, lambda l=l: V_.memset(kcar[l][:], 0.0), writes=[kcar[l]])
            fw.op(dve, lambda l=l: V_.memset(vcar[l][:], 0.0), writes=[vcar[l]])
            fw.op(dve, lambda l=l: V_.memset(ucar[l][:], 0.0), writes=[ucar[l]])
        for i in range(2):
            fw.op(dve, lambda i=i: V_.memset(kTz[i][:], 0.0), writes=[kTz[i]])
            fw.op(dve, lambda i=i: V_.memset(qsz[i][:], 0.0), writes=[qsz[i]])
        fw.op(pe, lambda: P_.matmul(P6[0:16, 0:384], setupA[:, 0:16], setupA[:, 144:528], start=True, stop=False),
              reads=[setupA], writes=[P6], inc=False)
        fw.op(pe, lambda: P_.matmul(P6[0:16, 0:384], setupB[:, 512:528], setupB[:, 0:384], start=False, stop=True),
              reads=[setupB], writes=[P6])
        fw.op(act, lambda: A_.activation(out=Tt[:], in_=P6[0:16, 0:384], func=AF.Copy), reads=[P6], writes=[Tt])
        fw.op(pe, lambda: P_.matmul(P5[:, 0:128], setupA[:, 16:144], setupA[:, 528:656], start=True, stop=True),
              reads=[setupA], writes=[P5])
        fw.op(act, lambda: A_.activation(out=Bd[:], in_=P5[:, 0:128], func=AF.Copy), reads=[P5], writes=[Bd])
        dst = bass.AP(scr, 0, [[128 * 385, 16], [385, 128], [1, 384]])
        fw.dma(sp, dst, Tt[:, :].unsqueeze(1).to_broadcast([16, 128, 384]), sem_scr, reads=[Tt], writes=[scrT])
        src = bass.AP(scr, 127, [[384, 128], [128 * 385, 16], [1, 256]])
        if 8 * (2 + GX) >= 4096:
            Bt = T("Bt_tmp", yT.ap[:, :, :].rearrange("p a b -> p (a b)")[:, 0:4096].rearrange("p (s k) -> p s k", s=16))
        else:
            Bt = T("Bt_tmp", es2.enter_context(nc.sbuf_tensor("sb_Bt_tmp", [128, 16, 256], F32)))
        fw.dma(sp, Bt[:], src, sem_scr, reads=[scrT], writes=[Bt])
        fw.op(dve, lambda: V_.tensor_scalar(out=Bhi[:], in0=Bt[:], scalar1=8.0, scalar2=None, op0=ALU.mult), reads=[Bt], writes=[Bhi])
        fw.op(dve, lambda: V_.scalar_tensor_tensor(out=Blo[:], in0=Bt[:], scalar=8.0, in1=Bhi[:], op0=ALU.mult, op1=ALU.subtract),
              reads=[Bt, Bhi], writes=[Blo])
        fw.inherit(yT, [Bt])
        fw.op(dve, lambda: V_.memset(mrow[:], 0.0), writes=[mrow])
        fw.op(dve, lambda: V_.tensor_scalar(out=mrow[:, 0:128], in0=maskadd[0:1, 0:1].to_broadcast([1, 128]), scalar1=8.0, scalar2=None, op0=ALU.mult),
              reads=[maskadd, mrow], writes=[mrow])
        es2.close()
        kvn = fw.sbuf("kvn", [NS, 512], F32)
        STd = T("STd", sbt[0].ap[:, 0, :])
        Sd = T("Sd", sbt[0].ap[:, 1, :].rearrange("p (h k) -> p h k", h=2))
        Pd = T("Pd", sbt[0].ap[:, 2, :].rearrange("p (h k) -> p h k", h=2))
        PTd = fw.sbuf("PTd", [128, NS * 16], BF16)
        kp_sb = fw.sbuf("kp_sb", [128, 2, 128], F32)
        vp_sb = fw.sbuf("vp_sb", [128, 256], F32)
        for t_ in (kvn, PTd, kp_sb, vp_sb):
            fw.inherit(t_, [setupA, setupB, Tt])

        ck = {"n": 0}

        def checkpoint(name=None):
            if name is not None:
                if isinstance(stop_after, str) and stop_after == "%d:%d:%s" % (ck.get("gi", 0), ck.get("l", 0), name):
                    raise _Stop()
                return
            ck["n"] += 1
            if isinstance(stop_after, int) and ck["n"] > stop_after:
                raise _Stop()

        def tiles_of(gi):
            gx = GX if gi == NG - 1 else G
            if G <= 512 and gx <= 512:
                return [(0, gx)]
            nt = (G + 383) // 384
            ts = []
            for t in range(nt):
                c0 = t * 384
                c1 = min(G, c0 + 384)
                if t == nt - 1:
                    c1 = gx
                ts.append((c0, c1 - c0))
            return ts

        def tile_of_block(gi, b):
            ts = tiles_of(gi)
            for t, (c0, n) in enumerate(ts):
                if c0 <= b * 128 < c0 + n:
                    return t
            raise AssertionError

        def gcol(l, i, c):
            k = (l * 6 + i) * 8 + c
            return k

        def norm_stats(ts_t, t, c0, n):
            for c in range(8):
                fw.op(pe, lambda c=c: P_.matmul(P6[:, 0:n], ones[:], bfB[:, c, c0:c0 + n], start=(c == 0), stop=(c == 7)),
                      reads=[ones, (bfB, (c, t))], writes=[P6], inc=(c == 7))
            fw.op(act, lambda: A_.activation(out=rstd[:, c0:c0 + n], in_=P6[:, 0:n], func=AF.Sqrt, scale=1.0 / D, bias=epst[:]),
                  reads=[P6, epst], writes=[(rstd, t)])
            fw.op(dve, lambda: V_.reciprocal(out=rstd[:, c0:c0 + n], in_=rstd[:, c0:c0 + n]), reads=[(rstd, t)], writes=[(rstd, t)])

        sqstate = {"ready": False}

        def pre_norm(gi, l, i):
            have_sq = sqstate["ready"]
            sqstate["ready"] = False
            for t, (c0, n) in enumerate(tiles_of(gi)):
                if not have_sq:
                    fw.op(act, lambda: A_.activation(out=bfB[:, :, c0:c0 + n], in_=xT[:, :, c0:c0 + n], func=AF.Square),
                          reads=[(xT, (c, t)) for c in range(8)], writes=[(bfB, (c, t)) for c in range(8)])
                norm_stats(None, t, c0, n)
                for c in range(8):
                    k = gcol(l, i, c)
                    fw.op(dve, lambda c=c, k=k: V_.scalar_tensor_tensor(out=hT[:, c, c0:c0 + n], in0=xT[:, c, c0:c0 + n],
                                                                       scalar=gT[:, k:k + 1], in1=rstd[:, c0:c0 + n],
                                                                       op0=ALU.mult, op1=ALU.mult),
                          reads=[(xT, (c, t)), gT, (rstd, t)], writes=[(hT, (c, t))])

        def post_norm(gi, l, i, half, presquare=False):
            gsrc = gH if half else gT
            sqstate["ready"] = presquare
            for t, (c0, n) in enumerate(tiles_of(gi)):
                norm_stats(None, t, c0, n)
                checkpoint("post%d_stats" % i)
                for c in range(8):
                    k = gcol(l, i, c)
                    checkpoint("post%d_c%d" % (i, c))
                    fw.op(dve, lambda c=c, k=k: V_.scalar_tensor_tensor(out=yT[:, c, 2 + c0:2 + c0 + n], in0=yT[:, c, 2 + c0:2 + c0 + n],
                                                                       scalar=gsrc[:, k:k + 1], in1=rstd[:, c0:c0 + n],
                                                                       op0=ALU.mult, op1=ALU.mult),
                          reads=[(yT, (c, t)), gsrc, (rstd, t)], writes=[(yT, (c, t))])
                    if c == 0:
                        checkpoint("post%d_stt" % i)
                    fw.op(dve, lambda c=c: V_.tensor_tensor(out=xT[:, c, c0:c0 + n], in0=xT[:, c, c0:c0 + n],
                                                            in1=yT[:, c, 2 + c0:2 + c0 + n], op=ALU.add),
                          reads=[(xT, (c, t)), (yT, (c, t))], writes=[(xT, (c, t))])
                    if presquare:
                        fw.op(act, lambda c=c: A_.activation(out=bfB[:, c, c0:c0 + n], in_=xT[:, c, c0:c0 + n], func=AF.Square),
                              reads=[(xT, (c, t))], writes=[(bfB, (c, t))])

        cnt = {"ps": 0, "po": 0, "tmp": 0}

        def next_ps():
            cnt["ps"] += 1
            return PSX[cnt["ps"] % 2]

        def next_ps4():
            cnt["ps"] += 1
            k = cnt["ps"] % 4
            return PSX[k % 2], ("h", k // 2), (k // 2) * 512

        def next_po():
            cnt["po"] += 1
            return POX[cnt["po"] % 2]

        def next_tmp():
            cnt["tmp"] += 1
            return tmpf[cnt["tmp"] % 3]

        def proj(ps_ap, ps_t, wslot, wv, j0, src, t, c0, n, last_inc=True):
            for kc in range(8):
                fw.op(pe, lambda kc=kc: P_.matmul(ps_ap, wv[:, kc, j0:j0 + 128], src[:, kc, c0:c0 + n], start=(kc == 0), stop=(kc == 7)),
                      reads=[wslot, (src, (kc, t))], writes=[ps_t], inc=(kc == 7 and last_inc))

        def dual_proj(gi, w, nj, func, dst, dst_off, dst_chunk0):
            sA, vA = w
            for t, (c0, n) in enumerate(tiles_of(gi)):
                for j in range(nj):
                    ps = next_ps()
                    proj(ps[:, 0:n], ps, sA, vA, j * 128, hT, t, c0, n, last_inc=False)
                    proj(ps[:, 512:512 + n], ps, sA, vA, 256 + j * 128, hT, t, c0, n)
                    tm = next_tmp()
                    fw.op(act, lambda: A_.activation(out=tm[:, 0:n], in_=ps[:, 0:n], func=func), reads=[ps], writes=[tm])
                    cch = dst_chunk0 + j
                    fw.op(dve, lambda: V_.tensor_tensor(out=dst[:, cch, dst_off + c0:dst_off + c0 + n], in0=tm[:, 0:n],
                                                        in1=ps[:, 512:512 + n], op=ALU.mult),
                          reads=[tm, ps], writes=[(dst, (cch, t))])

        def ffn(gi, l, ipre, ipost, presquare=False):
            ck["gi"], ck["l"] = gi, l
            checkpoint("ffn%d_start" % ipre)
            pre_norm(gi, l, ipre)
            checkpoint("ffn%d_pre" % ipre)
            for u in range(NFC // 2):
                dual_proj(gi, wnext(), 2, AF.Silu, actT, 0, 2 * u)
            checkpoint("ffn%d_dual" % ipre)
            for oc in range(8):
                ws, wv = wnext()
                for t, (c0, n) in enumerate(tiles_of(gi)):
                    po = next_po()
                    for fc in range(NFC):
                        fw.op(pe, lambda fc=fc: P_.matmul(po[:, 0:n], wv[:, fc, :], actT[:, fc, c0:c0 + n], start=(fc == 0), stop=(fc == NFC - 1)),
                              reads=[ws, (actT, (fc, t))], writes=[po], inc=(fc == NFC - 1))
                    fw.op(act, lambda: A_.activation(out=yT[:, oc, 2 + c0:2 + c0 + n], in_=po[:, 0:n], func=AF.Copy),
                          reads=[po], writes=[(yT, (oc, t))])
                    fw.op(act, lambda: A_.activation(out=bfB[:, oc, c0:c0 + n], in_=po[:, 0:n], func=AF.Square),
                          reads=[po], writes=[(bfB, (oc, t))])
            checkpoint("ffn%d_ph2" % ipre)
            post_norm(gi, l, ipost, True, presquare)

        def attn_stageA(gi, l, b, gp, hb, k, masked):
            c0q = b * 128
            tq = tile_of_block(gi, b)
            ps = PSX[k % 2]
            st_, pfb = st[k % 2], Pn[k % 2]
            ch0 = 4 * gp + 2 * hb
            s0 = 2 * ch0
            for i in range(4):
                cc, hh = divmod(i, 2)
                o_ = ps[:, i * 256:(i + 1) * 256]
                fw.op(pe, lambda cc=cc, hh=hh, o_=o_: P_.matmul(o_, bfB[:, ch0 + cc, c0q:c0q + 128], kTz[hh][:, gp, c0q:c0q + 256], start=True, stop=False),
                      reads=[(bfB, (ch0 + cc, tq)), kTz[hh]], writes=[ps], inc=False)
                fw.op(pe, lambda i=i, o_=o_: P_.matmul(o_, ident[:], Bhi[:, s0 + i, :], start=False, stop=False),
                      reads=[ident, Bhi], writes=[ps], inc=False)
                if masked:
                    fw.op(pe, lambda o_=o_: P_.matmul(o_, ones[0:1, :], mrow[:, :], start=False, stop=False),
                          reads=[ones, mrow], writes=[ps], inc=False)
                fw.op(pe, lambda i=i, o_=o_: P_.matmul(o_, ident[:], Blo[:, s0 + i, :], start=False, stop=True),
                      reads=[ident, Blo], writes=[ps], inc=(i == 3))
            fw.op(dve, lambda: V_.reduce_max(out=st_[:, 0:4], in_=ps[:, :].rearrange("p (h k) -> p h k", h=4), axis=AX.X), reads=[ps], writes=[st_])
            fw.op(dve, lambda: V_.scalar_tensor_tensor(out=st_[:, 4:8], in0=st_[:, 0:4], scalar=-0.125,
                                                       in1=nsinkb[:, l * 16 + s0:l * 16 + s0 + 4], op0=ALU.mult, op1=ALU.min),
                  reads=[st_, nsinkb], writes=[st_])
            fw.op(dve, lambda: V_.tensor_tensor(out=st_[:, 8:12], in0=st_[:, 4:8], in1=sinkb[:, l * 16 + s0:l * 16 + s0 + 4], op=ALU.add),
                  reads=[st_, sinkb], writes=[st_])
            for i in range(4):
                fw.op(act, lambda i=i: A_.activation(out=pfb[:, i, :], in_=ps[:, i * 256:(i + 1) * 256], func=AF.Exp, bias=st_[:, 4 + i:5 + i], scale=0.125,
                                                     accum_out=st_[:, 12 + i:13 + i]),
                      reads=[ps, (st_, "nm")], writes=[(pfb, i), (st_, ("rs", i))])
            fw.op(act, lambda: A_.activation(out=st_[:, 16:20], in_=st_[:, 8:12], func=AF.Exp), reads=[st_], writes=[st_])

        def attn_B1(gi, l, b, gp, hb, k):
            st_, pfb, dg = st[k % 2], Pn[k % 2], diag4[k % 2]
            fw.op(dve, lambda: V_.tensor_tensor(out=st_[:, 20:24], in0=st_[:, 12:16], in1=st_[:, 16:20], op=ALU.add), reads=[st_], writes=[st_])
            fw.op(dve, lambda: V_.reciprocal(out=st_[:, 24:28], in_=st_[:, 20:24]), reads=[st_], writes=[st_])
            fw.op(dve, lambda: V_.tensor_tensor(out=dg[:], in0=ident[:, :].unsqueeze(1).to_broadcast([128, 4, 128]),
                                                in1=st_[:, 24:28].unsqueeze(2).to_broadcast([128, 4, 128]), op=ALU.mult),
                  reads=[ident, st_], writes=[dg])
            for i in range(4):
                ptt = PTB if i < 2 else P6
                for kb in range(2):
                    j = (i % 2) * 2 + kb
                    fw.op(pe, lambda i=i, kb=kb, j=j, ptt=ptt: P_.matmul(ptt[:, j * 128:(j + 1) * 128], pfb[:, i, kb * 128:(kb + 1) * 128], dg[:, i, :],
                                                                         start=True, stop=True),
                          reads=[pfb, dg], writes=[ptt], inc=(j == 3))

        def attn_B2(gi, l, b, gp, hb, k):
            pts = PTs[k % 2]
            fw.op(dve, lambda: V_.tensor_copy(out=pts[:, 0:512], in_=PTB[:, :]), reads=[PTB], writes=[(pts, 0)])
            fw.op(dve, lambda: V_.tensor_copy(out=pts[:, 512:1024], in_=P6[:, :]), reads=[P6], writes=[(pts, 1)])

        def attn_B3(gi, l, b, gp, hb, k):
            pts, Vt, po = PTs[k % 2], Vl[l], POX[k % 2]
            for i in range(4):
                for kb in range(2):
                    j = i * 2 + kb
                    fw.op(pe, lambda i=i, kb=kb, j=j: P_.matmul(po[:, i * 128:(i + 1) * 128],
                                                                 Vt[:, b + kb, gp * 128:(gp + 1) * 128], pts[:, j * 128:(j + 1) * 128],
                                                                 start=(kb == 0), stop=(kb == 1)),
                          reads=[Vt, pts], writes=[po], inc=(j == 7))

        def attn_B4(gi, l, b, gp, hb, k):
            c0q = b * 128
            tq = tile_of_block(gi, b)
            po = POX[k % 2]
            ch0 = 4 * gp + 2 * hb
            for hh in range(2):
                fw.op(act, lambda hh=hh: A_.activation(out=bfA[hh * 64:(hh + 1) * 64, ch0:ch0 + 2, c0q:c0q + 128],
                                                       in_=po[hh * 64:(hh + 1) * 64, :].rearrange("p (c h q) -> p c h q", c=2, h=2)[:, :, hh, :], func=AF.Copy),
                      reads=[po], writes=[(bfA, (ch0 + c, tq)) for c in range(2)])

        def attention_prompt(gi, l):
            first_valid = divmod(2, NB)
            batches = [(b, gp, hb, k) for k, (b, gp, hb) in enumerate((b, gp, hb) for b in range(NB) for gp in range(2) for hb in range(2))]
            N = len(batches)
            for i in range(N + 3):
                if 0 <= i - 2 < N:
                    attn_B2(gi, l, *batches[i - 2])
                if 0 <= i - 3 < N:
                    attn_B4(gi, l, *batches[i - 3])
                if 0 <= i - 2 < N:
                    attn_B3(gi, l, *batches[i - 2])
                if 0 <= i - 1 < N:
                    attn_B1(gi, l, *batches[i - 1])
                if i < N:
                    b, gp, hb, k = batches[i]
                    attn_stageA(gi, l, b, gp, hb, k, masked=((gi, b) == first_valid))

        def attention_decode(l):
            tl = len(tiles_of(NG - 1)) - 1
            for hh in range(2):
                fw.op(dve, lambda hh=hh: V_.tensor_copy(out=KcT[hh * 64:(hh + 1) * 64, :, :, 0], in_=kTz[hh][hh * 64:(hh + 1) * 64, :, 128 + G:128 + GX]),
                      reads=[kTz[hh], KcT], writes=[KcT])
                fw.op(dve, lambda hh=hh: V_.tensor_copy(out=qsz[hh][hh * 64:(hh + 1) * 64, :, :], in_=bfB[hh * 64:(hh + 1) * 64, :, G:GX]),
                      reads=[(bfB, (c, tl)) for c in range(8)], writes=[qsz[hh]])
            n = 0
            for b in range(NS):
                for gp in range(2):
                    for gh in range(2):
                        n += 1
                        fw.op(pe, lambda b=b, gp=gp, gh=gh: P_.matmul(P6[:, b * 16 + 8 * gp + gh:b * 16 + 8 * gp + 8:2],
                                                                      KcT[:, gp, b, :],
                                                                      qsz[gh][:, 4 * gp:4 * gp + 4, b], start=True, stop=True),
                              reads=[KcT, qsz[gh]], writes=[P6], inc=(n == NS * 4))
            fw.op(act, lambda: A_.activation(out=STd[:], in_=P6[:, 0:NS * 16], func=AF.Copy), reads=[P6], writes=[STd])
            nh = (NS * 16) // 128
            for h in range(nh):
                fw.op(pe, lambda h=h: P_.transpose(P5[:, h * 128:(h + 1) * 128], STd[:, h * 128:(h + 1) * 128], identf[:]),
                      reads=[STd, identf], writes=[P5], inc=(h == nh - 1))
            st_ = st[0]
            fw.op(dve, lambda: V_.scalar_tensor_tensor(out=Sd[:], in0=P5[:, 0:nh * 128].rearrange("p (h k) -> p h k", h=nh), scalar=0.125,
                                                       in1=Bd[:, :].unsqueeze(1).to_broadcast([128, nh, 128]), op0=ALU.mult, op1=ALU.add),
                  reads=[P5, Bd], writes=[Sd])
            fw.op(dve, lambda: V_.reduce_max(out=st_[:, 0:nh], in_=Sd[:], axis=AX.X), reads=[Sd], writes=[st_])
            fw.op(dve, lambda: V_.tensor_scalar(out=st_[:, 4:4 + nh], in0=st_[:, 0:nh], scalar1=-1.0, scalar2=nsinkd[:, l:l + 1], op0=ALU.mult, op1=ALU.min),
                  reads=[st_, nsinkd], writes=[st_])
            fw.op(dve, lambda: V_.tensor_scalar(out=st_[:, 8:8 + nh], in0=st_[:, 4:4 + nh], scalar1=sinkd[:, l:l + 1], scalar2=None, op0=ALU.add),
                  reads=[st_, sinkd], writes=[st_])
            for h in range(nh):
                fw.op(act, lambda h=h: A_.activation(out=Pd[:, h, :], in_=Sd[:, h, :], func=AF.Exp, bias=st_[:, 4 + h:5 + h], scale=1.0,
                                                     accum_out=st_[:, 12 + h:13 + h]), reads=[Sd, st_], writes=[Pd, st_])
            fw.op(act, lambda: A_.activation(out=st_[:, 16:16 + nh], in_=st_[:, 8:8 + nh], func=AF.Exp), reads=[st_], writes=[st_])
            fw.op(dve, lambda: V_.tensor_tensor(out=st_[:, 20:20 + nh], in0=st_[:, 12:12 + nh], in1=st_[:, 16:16 + nh], op=ALU.add), reads=[st_], writes=[st_])
            fw.op(dve, lambda: V_.reciprocal(out=st_[:, 24:24 + nh], in_=st_[:, 20:20 + nh]), reads=[st_], writes=[st_])
            fw.op(dve, lambda: V_.tensor_tensor(out=Pd[:], in0=Pd[:], in1=st_[:, 24:24 + nh].unsqueeze(2).to_broadcast([128, nh, 128]), op=ALU.mult),
                  reads=[Pd, st_], writes=[Pd])
            for h in range(nh):
                fw.op(pe, lambda h=h: P_.transpose(P4[:, h * 128:(h + 1) * 128], Pd[:, h, :], identf[:]),
                      reads=[Pd, identf], writes=[P4], inc=(h == nh - 1))
            fw.op(act, lambda: A_.activation(out=PTd[:], in_=P4[:, 0:nh * 128], func=AF.Copy), reads=[P4], writes=[PTd])
            n = 0
            for b in range(NS):
                for gp in range(2):
                    for gh in range(2):
                        n += 1
                        base = (gh * 8 + gp * 4) * NS
                        fw.op(pe, lambda b=b, gp=gp, gh=gh, base=base: P_.matmul(P5[:, base + b:base + 4 * NS + b:NS],
                                                                                 Vc[:, b, gp * 128:(gp + 1) * 128],
                                                                                 PTd[:, b * 16 + 8 * gp + gh:b * 16 + 8 * gp + 8:2], start=True, stop=True),
                              reads=[Vc, PTd], writes=[P5], inc=(n == NS * 4))
            for gh in range(2):
                fw.op(act, lambda gh=gh: A_.activation(out=bfA[gh * 64:(gh + 1) * 64, :, G:GX],
                                                       in_=P5[gh * 64:(gh + 1) * 64, gh * 8 * NS:(gh + 1) * 8 * NS].rearrange("p (c b) -> p c b", c=8), func=AF.Copy),
                      reads=[P5], writes=[(bfA, (c, tl)) for c in range(8)])

        def sq_proj(gi, wlist, src, epilogue):
            for hf in range(2):
                ws, wv = wnext()
                for j in range(4):
                    oc = hf * 4 + j
                    checkpoint()
                    for t, (c0, n) in enumerate(tiles_of(gi)):
                        po = next_po()
                        proj(po[:, 0:n], po, ws, wv, j * 128, src, t, c0, n)
                        epilogue(oc, t, c0, n, po)

        def mixer(gi, l):
            last = (gi == NG - 1)
            Vt = Vl[l]
            ts = tiles_of(gi)
            tl = len(ts) - 1
            sigA = T("sigA", actT.ap)
            sigB = T("sigB", actT.ap)
            fw.inherit(sigA, [actT])
            fw.inherit(sigB, [actT])
            ck["gi"], ck["l"] = gi, l
            pre_norm(gi, l, 2)
            checkpoint("mix_pre")
            if last:
                fw.inherit(KcT, [actT, sigA, sigB])
                fw.dma(pool, KcT[:], kcT_d[l], sem_kc, writes=[KcT])
                fw.dma(sp, stT[:], stT_d[l], sem_st, writes=[stT])
            checkpoint("mix_loads")
            fw.op(dve, lambda: V_.tensor_copy(out=yT[:, :, 0:2], in_=ucar[l][:]), reads=[ucar[l]], writes=[(yT, (c, "car")) for c in range(8)])
            for hh in range(2):
                fw.op(dve, lambda hh=hh: V_.tensor_copy(out=kTz[hh][hh * 64:(hh + 1) * 64, :, 0:128], in_=kcar[l][hh * 64:(hh + 1) * 64, :, :]),
                      reads=[kcar[l]], writes=[kTz[hh]])
            fw.op(dve, lambda: V_.tensor_copy(out=Vt[:, 0, :], in_=vcar[l][:]), reads=[vcar[l]], writes=[Vt])
            checkpoint("mix_carry")
            for uu in range(4):
                dual_proj(gi, wnext(), 2, AF.Copy, yT, 2, 2 * uu)
                checkpoint("mix_u%d" % uu)
            checkpoint()
            for uu in range(2):
                ws, wv = wnext()
                for j in range(4):
                    c = 4 * uu + j
                    for t, (c0, n) in enumerate(ts):
                        npr = (G - c0) if (last and t == tl) else n
                        tm = next_tmp()
                        rd = [(yT, (c, t)), cwT] + ([(yT, (c, t - 1))] if t > 0 else [(yT, (c, "car"))])
                        w0, w1, w2 = [cwT[:, (l * 3 + i) * 8 + c:(l * 3 + i) * 8 + c + 1] for i in range(3)]
                        fw.op(dve, lambda: V_.tensor_scalar(out=tm[:, 0:npr], in0=yT[:, c, c0:c0 + npr], scalar1=w0, scalar2=None, op0=ALU.mult),
                              reads=rd, writes=[tm])
                        fw.op(dve, lambda: V_.scalar_tensor_tensor(out=tm[:, 0:npr], in0=yT[:, c, c0 + 1:c0 + 1 + npr], scalar=w1, in1=tm[:, 0:npr],
                                                                   op0=ALU.mult, op1=ALU.add), reads=rd + [tm], writes=[tm])
                        fw.op(dve, lambda: V_.scalar_tensor_tensor(out=tm[:, 0:npr], in0=yT[:, c, c0 + 2:c0 + 2 + npr], scalar=w2, in1=tm[:, 0:npr],
                                                                   op0=ALU.mult, op1=ALU.add), reads=rd + [tm], writes=[tm])
                        if last and t == tl:
                            fw.op(dve, lambda: V_.tensor_scalar(out=tm[:, npr:n], in0=stT[:, c, 0, :], scalar1=w0, scalar2=None, op0=ALU.mult),
                                  reads=[stT, cwT, tm], writes=[tm])
                            fw.op(dve, lambda: V_.scalar_tensor_tensor(out=tm[:, npr:n], in0=stT[:, c, 1, :], scalar=w1, in1=tm[:, npr:n],
                                                                       op0=ALU.mult, op1=ALU.add), reads=[stT, cwT, tm], writes=[tm])
                            fw.op(dve, lambda: V_.scalar_tensor_tensor(out=tm[:, npr:n], in0=yT[:, c, 2 + G:2 + GX], scalar=w2, in1=tm[:, npr:n],
                                                                       op0=ALU.mult, op1=ALU.add), reads=rd + [tm], writes=[tm])
                        ps, pk, po_ = next_ps4()
                        proj(ps[:, po_:po_ + n], (ps, pk), ws, wv, j * 128, hT, t, c0, n)
                        fw.op(dve, lambda: V_.tensor_tensor(out=bfA[:, c, c0:c0 + n], in0=tm[:, 0:n], in1=ps[:, po_:po_ + n], op=ALU.mult),
                              reads=[tm, (ps, pk)], writes=[(bfA, (c, t))])
            checkpoint()
            fw.op(dve, lambda: V_.tensor_copy(out=ucar[l][:], in_=yT[:, :, G:G + 2]), reads=[(yT, (c, tile_of_block(gi, NB - 1))) for c in range(8)], writes=[ucar[l]])
            if last:
                fw.dma(sp, convp_o[l], ucar[l][:], sem_o[0], reads=[ucar[l]])
                fw.dma(sp, convs_o[l][:, :, 0, :], stT[:, :, 1, :], sem_o[1], reads=[stT])
                fw.dma(sp, convs_o[l][:, :, 1, :], yT[:, :, 2 + G:2 + GX], sem_o[6], reads=[(yT, (c, tl)) for c in range(8)])
            checkpoint("mix_cout")
            for uu in range(2):
                ws, wv = wnext()
                for j in range(4):
                    c = 4 * uu + j
                    for t, (c0, n) in enumerate(ts):
                        ps, pk, po_ = next_ps4()
                        proj(ps[:, po_:po_ + n], (ps, pk), ws, wv, j * 128, hT, t, c0, n)
                        fw.op(act, lambda: A_.activation(out=bfB[:, c, c0:c0 + n], in_=ps[:, po_:po_ + n], func=AF.Copy), reads=[(ps, pk)], writes=[(bfB, (c, t))])
            checkpoint("mix_q")
            ws, wv = wnext()
            for gp in range(2):
                for t, (c0, n) in enumerate(ts):
                    ps, pk, po_ = next_ps4()
                    proj(ps[:, po_:po_ + n], (ps, pk), ws, wv, gp * 128, hT, t, c0, n)
                    for hh in range(2):
                        fw.op(act, lambda hh=hh: A_.activation(out=kTz[hh][hh * 64:(hh + 1) * 64, gp, 128 + c0:128 + c0 + n], in_=ps[hh * 64:(hh + 1) * 64, po_:po_ + n], func=AF.Copy),
                              reads=[(ps, pk)], writes=[kTz[hh]])
                    if last and t == tl:
                        o0 = (NB - 1) * 128 - c0
                        fw.op(act, lambda: A_.activation(out=kp_sb[:, gp, :], in_=ps[:, po_ + o0:po_ + o0 + 128], func=AF.Copy), reads=[(ps, pk)], writes=[kp_sb])
            checkpoint("mix_k")
            for b in range(NB):
                tb = tile_of_block(gi, b)
                ps = next_ps()
                for kc in range(8):
                    fw.op(pe, lambda kc=kc: P_.matmul(ps[:, 0:256], hT[:, kc, b * 128:(b + 1) * 128], wv[:, kc, 256:512], start=(kc == 0), stop=(kc == 7)),
                          reads=[ws, (hT, (kc, tb))], writes=[ps], inc=(kc == 7))
                fw.op(act, lambda: A_.activation(out=Vt[:, b + 1, :], in_=ps[:, 0:256], func=AF.Copy), reads=[ps], writes=[Vt])
                if last and b == NB - 1:
                    fw.op(act, lambda: A_.activation(out=vp_sb[:], in_=ps[:, 0:256], func=AF.Copy), reads=[ps], writes=[vp_sb])
            checkpoint("mix_v")
            if last:
                fw.dma(sp, kpT_o[l], kp_sb[:], sem_o[2], reads=[kp_sb])
                fw.dma(sp, vp_o[l], vp_sb[:], sem_o[3], reads=[vp_sb])
                ps = next_ps()
                for kc in range(8):
                    fw.op(pe, lambda kc=kc: P_.matmul(ps[0:NS, 0:512], hT[:, kc, G:GX], wv[:, kc, 0:512], start=(kc == 0), stop=(kc == 7)),
                          reads=[ws, (hT, (kc, tl))], writes=[ps], inc=(kc == 7))
                checkpoint("mix_kpvp")
                fw.op(act, lambda: A_.activation(out=kvn[:], in_=ps[0:NS, 0:512], func=AF.Copy), reads=[ps], writes=[kvn])
                checkpoint("mix_kvn")
                fw.dma(sp, ks_o[l, :, 127, :], kvn[:, 0:256], sem_o[4], reads=[kvn], writes=[(ksT, ("n", l))])
                fw.dma(sp, vs_o[l, :, 127, :], kvn[:, 256:512], sem_o[5], reads=[kvn], writes=[(vsT, ("n", l))])
            checkpoint()
            for which, dstv in ((0, sigA), (1, sigB)):
                for uu in range(2):
                    ws, wv = wnext()
                    for j in range(4):
                        c = 4 * uu + j
                        for t, (c0, n) in enumerate(ts):
                            ps, pk, po_ = next_ps4()
                            proj(ps[:, po_:po_ + n], (ps, pk), ws, wv, j * 128, hT, t, c0, n)
                            fw.op(act, lambda: A_.activation(out=actT[:, 8 * which + c, c0:c0 + n], in_=ps[:, po_:po_ + n], func=AF.Sigmoid),
                                  reads=[(ps, pk)], writes=[(dstv, (c, t))])
            checkpoint()
            wl = None

            def ep_co(oc, t, c0, n, po):
                fw.op(dve, lambda: V_.tensor_tensor(out=actT[:, oc, c0:c0 + n], in0=actT[:, oc, c0:c0 + n], in1=po[:, 0:n], op=ALU.mult),
                      reads=[(sigA, (oc, t)), po], writes=[(sigA, (oc, t))])
            sq_proj(gi, wl, bfA, ep_co)
            checkpoint()
            attention_prompt(gi, l)
            checkpoint()
            if last:
                fw.inherit(Vc, Pn + PTs)
                fw.dma(pool, Vc[:], vc_d[l].rearrange("b j f -> j b f"), sem_vc, writes=[Vc])
                vsrc = vs_o[l, :, 127:128, :].rearrange("b o f -> o b f")
                fw.dma(pool, Vc[0:1, :, :], vsrc, sem_v0, reads=[(vsT, ("n", l))], writes=[Vc])
                for t_ in (STd, Sd, Pd):
                    fw.inherit(t_, [sbt[0]])
                attention_decode(l)
                for t_ in Pn + PTs:
                    fw.inherit(t_, [Vc])
                fw.inherit(sbt[0], [sbt[0], STd, Sd, Pd])
            if not last:
                for hh in range(2):
                    fw.op(dve, lambda hh=hh: V_.tensor_copy(out=kcar[l][hh * 64:(hh + 1) * 64, :, :], in_=kTz[hh][hh * 64:(hh + 1) * 64, :, G:G + 128]),
                          reads=[kTz[hh]], writes=[kcar[l]])
                fw.op(dve, lambda: V_.tensor_copy(out=vcar[l][:], in_=Vt[:, NB, :]), reads=[Vt], writes=[vcar[l]])
            wl = None

            def ep_ao(oc, t, c0, n, po):
                fw.op(dve, lambda: V_.tensor_tensor(out=actT[:, 8 + oc, c0:c0 + n], in0=actT[:, 8 + oc, c0:c0 + n], in1=po[:, 0:n], op=ALU.mult),
                      reads=[(sigB, (oc, t)), po], writes=[(sigB, (oc, t))])
                fw.op(dve, lambda: V_.tensor_tensor(out=hT[:, oc, c0:c0 + n], in0=actT[:, 8 + oc, c0:c0 + n], in1=actT[:, oc, c0:c0 + n], op=ALU.add),
                      reads=[(sigB, (oc, t)), (sigA, (oc, t))], writes=[(hT, (oc, t))])
            sq_proj(gi, wl, bfA, ep_ao)
            checkpoint()
            wl = None

            def ep_wo(oc, t, c0, n, po):
                fw.op(act, lambda: A_.activation(out=yT[:, oc, 2 + c0:2 + c0 + n], in_=po[:, 0:n], func=AF.Copy), reads=[po], writes=[(yT, (oc, t))])
                fw.op(act, lambda: A_.activation(out=bfB[:, oc, c0:c0 + n], in_=po[:, 0:n], func=AF.Square), reads=[po], writes=[(bfB, (oc, t))])
            sq_proj(gi, wl, hT, ep_wo)
            post_norm(gi, l, 3, False, True)
            fw.inherit(actT, [sigA, sigB, actT] + ([KcT] if last else []))

        xv = xT_d.rearrange("(c p) n -> p c n", p=128)
        yv = yT_o.rearrange("(c p) n -> p c n", p=128)
        def main_program():
          checkpoint()
          for gi in range(NG):
              last = (gi == NG - 1)
              fw.dma(sp, xT[:, :, 0:G], xv[:, :, gi * G:(gi + 1) * G], sem_x, writes=[xT])
              if last:
                  fw.dma(sp, xT[:, :, G:GX], xsT_d.rearrange("(c p) n -> p c n", p=128), sem_x, writes=[xT])
              for l in range(L):
                  ffn(gi, l, 0, 1, True)
                  checkpoint()
                  mixer(gi, l)
                  checkpoint()
                  ffn(gi, l, 4, 5, (l < L - 1) and (gi * G >= 256))
                  checkpoint()
                  h0 = gi * G
                  if h0 < 256:
                      hn = min(256 - h0, G)
                      fw.op(dve, lambda: V_.tensor_scalar(out=xT[:, :, 0:hn], in0=xT[:, :, 0:hn], scalar1=flag[:, 0:1], scalar2=None, op0=ALU.mult),
                            reads=[xT, flag], writes=[xT])
              v0 = max(256 - gi * G, 0)
              if v0 < G:
                  fw.dma(sp, yv[:, :, gi * G + v0 - 256:(gi + 1) * G - 256], xT[:, :, v0:G], sem_y, reads=[xT])
              if last:
                  fw.dma(sp, ysT_o.rearrange("(c p) n -> p c n", p=128), xT[:, :, G:GX], sem_y, reads=[xT])
        try:
            main_program()
            assert wstate["used"] == len(plan), (wstate, len(plan))
        except _Stop:
            pass
        fw.wait_all(sp, fw.dma_sems)
        build.stats = dict(ninstr=fw.ninstr, pe=pe.sem.n, act=act.sem.n, dve=dve.sem.n, nsem=fw.nsem)
    return nc


def static_tables():
    j = np.arange(384)
    rel = 255 - j
    valid = (rel >= 0) & (rel < 128) & (j < 383)
    oh = np.zeros((32, 384), np.float32)
    bk = t5_bucket_np(np.clip(rel, 0, None))
    oh[bk[valid], j[valid]] = 1.0
    negm = np.where(valid, 0.0, NEGBIG).astype(np.float32)[None, :]
    jj = np.arange(128)
    reld = np.where(jj == 0, 0, 128 - jj)
    ohdec = np.zeros((32, 128), np.float32)
    ohdec[t5_bucket_np(reld), jj] = 1.0
    negmdec = np.zeros((1, 128), np.float32)
    return oh, negm, ohdec, negmdec


def make_in_maps(cfg, n_cores, cores_per_seq, inp):
    L, NS, SPAN, NV = cfg.L, cfg.NS, cfg.SPAN, cfg.NV
    perm = np.array([slot_head(s) for s in range(16)])
    qcols = (perm[:, None] * 64 + np.arange(64)[None, :]).reshape(-1)
    win = np.array(inp["w_in"], dtype=np.float32, copy=True)
    win[:, :, 3072:4096] = win[:, :, 3072 + qcols]
    wao = np.ascontiguousarray(np.asarray(inp["w_attn_out"], np.float32)[:, qcols, :])
    relb = np.ascontiguousarray(np.asarray(inp["rel_bias"], np.float32)[:, perm])
    relbrep = np.ascontiguousarray(np.tile(relb, (1, 8)))
    sinks = np.asarray(inp["sinks"], np.float32)[:, perm]
    sinkb = np.ascontiguousarray(np.broadcast_to(sinks.reshape(1, L * 16), (128, L * 16)))
    sinkd = np.ascontiguousarray(np.tile(sinks.T, (8, 1)))
    g = np.asarray(inp["norm_g"], np.float32)
    gT = np.ascontiguousarray(g.reshape(L, 6, 8, 128).transpose(3, 0, 1, 2).reshape(128, L * 48))
    cw = np.asarray(inp["conv_w"], np.float32)
    cwT = np.ascontiguousarray(cw.reshape(L, 3, 8, 128).transpose(3, 0, 1, 2).reshape(128, L * 24))
    oh, negm, ohdec, negmdec = static_tables()
    shared = dict(relb=relb, relbrep=relbrep, oh=oh, ohdec=ohdec, negm=negm, negmdec=negmdec, gT=gT, cwT=cwT,
                  sinkb=sinkb, sinkd=sinkd,
                  w1gu=np.ascontiguousarray(inp["w_ff1_gu"], dtype=np.float32), w1d=np.ascontiguousarray(inp["w_ff1_down"], dtype=np.float32),
                  win=win, wco=np.ascontiguousarray(inp["w_conv_out"], dtype=np.float32), wao=wao,
                  wo=np.ascontiguousarray(inp["w_out"], dtype=np.float32),
                  w2gu=np.ascontiguousarray(inp["w_ff2_gu"], dtype=np.float32), w2d=np.ascontiguousarray(inp["w_ff2_down"], dtype=np.float32))
    xp = np.asarray(inp["x_prompt"], np.float32)
    xs = np.asarray(inp["x_sample"], np.float32)
    stc = np.asarray(inp["state_conv"], np.float32)
    ck = np.asarray(inp["cache_k_win"], np.float32)
    cv = np.asarray(inp["cache_v_win"], np.float32)
    maps = []
    for core in range(n_cores):
        sq_, pos = divmod(core, cores_per_seq)
        s0 = pos * NV * 128
        xT = np.zeros((D, SPAN), np.float32)
        lo = s0 - 256
        if lo >= 0:
            xT[:, :] = xp[sq_, lo:lo + SPAN, :].T
        else:
            xT[:, 256:] = xp[sq_, 0:SPAN - 256, :].T
        b0 = core * NS
        m = dict(shared)
        m["xT"] = xT
        m["xsT"] = np.ascontiguousarray(xs[b0:b0 + NS, 0, :].T)
        m["flag"] = np.full((128, 1), 1.0 if lo >= 0 else 0.0, np.float32)
        m["stT"] = np.ascontiguousarray(stc[:, b0:b0 + NS].reshape(L, NS, 2, 8, 128).transpose(0, 4, 3, 2, 1))
        kk = ck[:, b0:b0 + NS].reshape(L, NS, 128, 2, 2, 64)
        m["kcT"] = np.ascontiguousarray(kk.transpose(0, 4, 5, 3, 1, 2).reshape(L, 128, 2, NS, 128))
        m["kc"] = np.ascontiguousarray(ck[:, b0:b0 + NS].reshape(L, NS, 128, 256))
        m["vc"] = np.ascontiguousarray(cv[:, b0:b0 + NS].reshape(L, NS, 128, 256))
        maps.append(m)
    return maps


def assemble(cfg, n_cores, cores_per_seq, res, n_seq):
    L, NS, NV = cfg.L, cfg.NS, cfg.NV
    seq = cores_per_seq * NV * 128
    yp = np.zeros((n_seq, seq, D), np.float32)
    ys = np.zeros((n_cores * NS, 1, D), np.float32)
    pc = np.zeros((L, n_seq, 2, D), np.float32)
    pk = np.zeros((L, n_seq, 128, NKV, HD), np.float32)
    pv = np.zeros((L, n_seq, 128, NKV, HD), np.float32)
    sc = np.zeros((L, n_cores * NS, 2, D), np.float32)
    sk = np.zeros((L, n_cores * NS, 128, NKV, HD), np.float32)
    sv = np.zeros((L, n_cores * NS, 128, NKV, HD), np.float32)
    for core in range(n_cores):
        r = res[core]
        sq_, pos = divmod(core, cores_per_seq)
        s0 = pos * NV * 128
        yp[sq_, s0:s0 + NV * 128, :] = np.asarray(r["yT"]).T
        b0 = core * NS
        ys[b0:b0 + NS, 0, :] = np.asarray(r["ysT"]).T
        sc[:, b0:b0 + NS] = np.asarray(r["convs"]).transpose(0, 4, 3, 2, 1).reshape(L, NS, 2, D)
        sk[:, b0:b0 + NS] = np.asarray(r["ks"]).reshape(L, NS, 128, NKV, HD)
        sv[:, b0:b0 + NS] = np.asarray(r["vs"]).reshape(L, NS, 128, NKV, HD)
        if pos == cores_per_seq - 1:
            pc[:, sq_] = np.asarray(r["convp"]).transpose(0, 3, 2, 1).reshape(L, 2, D)
            kp = np.asarray(r["kpT"]).reshape(L, 2, 64, 2, 128)
            pk[:, sq_] = kp.transpose(0, 4, 3, 1, 2).reshape(L, 128, NKV, HD)
            pv[:, sq_] = np.asarray(r["vp"]).reshape(L, 128, NKV, HD)
    return (yp, ys, pc, pk, pv, sc, sk, sv)


_CACHE = {}


def kernel(**inputs):
    cfg = Cfg()
    if "nc" not in _CACHE:
        _CACHE["nc"] = build(cfg)
    nc = _CACHE["nc"]
    in_maps = make_in_maps(cfg, 8, 4, inputs)
    res = run_bass_kernel_spmd(nc, in_maps, core_ids=list(range(8)))
    return assemble(cfg, 8, 4, res.results, 2)
```

```python
import math
from contextlib import ExitStack
import numpy as np
import concourse.bass as bass
import concourse.mybir as mybir
from concourse.bass_utils import run_bass_kernel_spmd

F32 = mybir.dt.float32
BF16 = mybir.dt.bfloat16
AF = mybir.ActivationFunctionType
ALU = mybir.AluOpType
AX = mybir.AxisListType

D = 1024
NH = 16
NKV = 4
HD = 64
N_BUCKETS = 32
MAX_DISTANCE = 128
IN_W = 6656
EPS = 1e-6
NEGBIG = -1.0e9


class Sem:
    def __init__(self, handle):
        self.h = handle
        self.n = 0


class Eng:
    def __init__(self, name, handle, sem, is_pe=False):
        self.name = name
        self.h = handle
        self.sem = sem
        self.known = {}
        self.is_pe = is_pe


class Rec:
    __slots__ = ("w", "r")

    def __init__(self):
        self.w = {}
        self.r = {}


class T:
    def __init__(self, name, ap):
        self.name = name
        self.ap = ap
        self.recs = {None: Rec()}

    def __getitem__(self, k):
        return self.ap[k]


def _acc(x):
    return x if isinstance(x, tuple) else (x, None)


def _put(d, sv):
    s, v = sv
    k = id(s)
    if k not in d or d[k][1] < v:
        d[k] = (s, v)


class FW:
    def __init__(self, nc, es):
        self.nc = nc
        self.es = es
        self.nsem = 0
        self.dma_sems = []
        self.pe = Eng("pe", nc.tensor, self.new_sem(False), is_pe=True)
        self.act = Eng("act", nc.scalar, self.new_sem(False))
        self.dve = Eng("dve", nc.vector, self.new_sem(False))
        self.pool = Eng("pool", nc.gpsimd, self.new_sem(False))
        self.sp = Eng("sp", nc.sync, self.new_sem(False))
        self.ninstr = 0

    def new_sem(self, dma=True):
        self.nsem += 1
        h = self.es.enter_context(self.nc.semaphore("s%d" % self.nsem))
        s = Sem(h)
        if dma:
            self.dma_sems.append(s)
        return s

    def sbuf(self, name, shape, dtype):
        return T(name, self.es.enter_context(self.nc.sbuf_tensor("sb_" + name, shape, dtype)))

    def psum(self, name, shape, dtype):
        return T(name, self.es.enter_context(self.nc.psum_tensor("ps_" + name, shape, dtype)))

    def _recs(self, t, key):
        if key is None:
            return list(t.recs.values())
        out = [t.recs[None]]
        if key in t.recs:
            out.append(t.recs[key])
        return out

    def _needs(self, reads, writes):
        needs = {}
        for a in reads:
            t, key = _acc(a)
            for rec in self._recs(t, key):
                for sv in rec.w.values():
                    _put(needs, sv)
        for a in writes:
            t, key = _acc(a)
            for rec in self._recs(t, key):
                for sv in rec.w.values():
                    _put(needs, sv)
                for sv in rec.r.values():
                    _put(needs, sv)
        return needs

    def _wait(self, eng, needs):
        for k, (s, v) in needs.items():
            if s is eng.sem and eng.is_pe:
                continue
            if eng.known.get(k, 0) >= v:
                continue
            eng.h.wait_ge(s.h, v)
            eng.known[k] = v
            self.ninstr += 1

    def _mark(self, sv, reads, writes):
        wset = set()
        for a in writes:
            t, key = _acc(a)
            wset.add((id(t), key))
            if key is None:
                rec = Rec()
                t.recs = {None: rec}
            else:
                rec = t.recs.get(key)
                if rec is None:
                    rec = t.recs[key] = Rec()
                rec.w = {}
                rec.r = {}
            _put(rec.w, sv)
        for a in reads:
            t, key = _acc(a)
            if (id(t), key) in wset:
                continue
            if key is None:
                for rec in t.recs.values():
                    _put(rec.r, sv)
            else:
                rec = t.recs.get(key)
                if rec is None:
                    rec = t.recs[key] = Rec()
                _put(rec.r, sv)

    def op(self, eng, fn, reads=(), writes=(), inc=True):
        self._wait(eng, self._needs(reads, writes))
        ins = fn()
        self.ninstr += 1
        if inc:
            eng.sem.n += 1
            ins.then_inc(eng.sem.h, 1)
            v = eng.sem.n
        else:
            v = eng.sem.n + 1
        self._mark((eng.sem, v), reads, writes)
        return ins

    def dma(self, q, out_ap, in_ap, sem, reads=(), writes=(), **kw):
        self._wait(q, self._needs(reads, writes))
        ins = q.h.dma_start(out=out_ap, in_=in_ap, **kw)
        sem.n += 16
        ins.then_inc(sem.h, 16)
        self.ninstr += 1
        self._mark((sem, sem.n), reads, writes)
        return ins

    def inherit(self, dst, srcs):
        rec = Rec()
        for s in srcs:
            for r in s.recs.values():
                for sv in r.w.values():
                    _put(rec.r, sv)
                for sv in r.r.values():
                    _put(rec.r, sv)
        dst.recs = {None: rec}

    def wait_all(self, eng, sems):
        for s in sems:
            if s.n > 0 and eng.known.get(id(s), 0) < s.n:
                eng.h.wait_ge(s.h, s.n)
                eng.known[id(s)] = s.n


def slot_head(s):
    c, hh = divmod(s, 2)
    g = 2 * (c // 4) + hh
    return 4 * g + (c % 4)


def t5_bucket_np(rel):
    n = np.maximum(rel, 0)
    max_exact = N_BUCKETS // 2
    nf = np.maximum(n, 1).astype(np.float32)
    large = max_exact + (np.log(nf / np.float32(max_exact)) / np.float32(math.log(MAX_DISTANCE / max_exact))
                         * np.float32(N_BUCKETS - max_exact)).astype(np.int32)
    large = np.minimum(large, N_BUCKETS - 1)
    return np.where(n < max_exact, n, large)


class Cfg:
    def __init__(self, NG=3, NB=6, DFF=2816, L=2, NS=16, R=3):
        self.NG, self.NB, self.DFF, self.L, self.NS, self.R = NG, NB, DFF, L, NS, R
        self.NBT = NG * NB
        self.NV = self.NBT - 2
        self.G = NB * 128
        self.GX = self.G + NS
        self.NFC = DFF // 128
        self.SPAN = self.NBT * 128


class _Stop(Exception):
    pass


def build(cfg, stop_after=None):
    NG, NB, DFF, L, NS, R = cfg.NG, cfg.NB, cfg.DFF, cfg.L, cfg.NS, cfg.R
    G, GX, NFC, SPAN, NV = cfg.G, cfg.GX, cfg.NFC, cfg.SPAN, cfg.NV
    nc = bass.Bass("TRN2", target_bir_lowering=False)

    def din(name, shape):
        return nc.dram_tensor(name, list(shape), F32, kind="ExternalInput").ap()

    def dout(name, shape):
        return nc.dram_tensor(name, list(shape), F32, kind="ExternalOutput").ap()

    xT_d = din("xT", [D, SPAN])
    xsT_d = din("xsT", [D, NS])
    flag_d = din("flag", [128, 1])
    stT_d = din("stT", [L, 128, 8, 2, NS])
    kcT_d = din("kcT", [L, 128, 2, NS, 128])
    kc_d = din("kc", [L, NS, 128, 256])
    vc_d = din("vc", [L, NS, 128, 256])
    relb_d = din("relb", [32, 16])
    relbrep_d = din("relbrep", [32, 128])
    oh_d = din("oh", [32, 384])
    ohdec_d = din("ohdec", [32, 128])
    negm_d = din("negm", [1, 384])
    negmdec_d = din("negmdec", [1, 128])
    gT_d = din("gT", [128, L * 6 * 8])
    cwT_d = din("cwT", [128, L * 3 * 8])
    sinkb_d = din("sinkb", [128, L * 16])
    sinkd_d = din("sinkd", [128, L])
    w1gu_d = din("w1gu", [L, D, 2 * DFF])
    w1d_d = din("w1d", [L, DFF, D])
    win_d = din("win", [L, D, IN_W])
    wco_d = din("wco", [L, D, D])
    wao_d = din("wao", [L, D, D])
    wo_d = din("wo", [L, D, D])
    w2gu_d = din("w2gu", [L, D, 2 * DFF])
    w2d_d = din("w2d", [L, DFF, D])

    yT_o = dout("yT", [D, NV * 128])
    ysT_o = dout("ysT", [D, NS])
    convp_o = dout("convp", [L, 128, 8, 2])
    kpT_o = dout("kpT", [L, 128, 2, 128])
    vp_o = dout("vp", [L, 128, 256])
    convs_o = dout("convs", [L, 128, 8, 2, NS])
    ks_o = dout("ks", [L, NS, 128, 256])
    vs_o = dout("vs", [L, NS, 128, 256])
    scr = nc.dram_tensor("scr", [16 * 128 * 385 + 1024], F32, kind="Internal")

    with ExitStack() as es:
        fw = FW(nc, es)
        pe, act, dve, pool, sp = fw.pe, fw.act, fw.dve, fw.pool, fw.sp
        V_, A_, P_ = nc.vector, nc.scalar, nc.tensor

        xT = fw.sbuf("xT", [128, 8, GX], F32)
        hT = fw.sbuf("hT", [128, 8, GX], BF16)
        bfB = fw.sbuf("bfB", [128, 8, GX], BF16)
        rstd = fw.sbuf("rstd", [128, GX], F32)
        NKC = (4096 + GX - 1) // GX
        actT = fw.sbuf("actT", [128, max(NFC, 16 + NKC), GX], BF16)
        yT = fw.sbuf("yT", [128, 8, 2 + GX], F32)
        bfA = fw.sbuf("bfA", [128, 8, GX], BF16)
        kTz = [fw.sbuf("kTz%d" % i, [128, 2, 128 + GX], BF16) for i in range(2)]
        qsz = [fw.sbuf("qsz%d" % i, [128, 8, NS], BF16) for i in range(2)]
        Vt1 = fw.sbuf("Vt", [128, NB + 1, 256], BF16)
        Vl = [Vt1] * L
        kcar = [fw.sbuf("kcar%d" % l, [128, 2, 128], BF16) for l in range(L)]
        vcar = [fw.sbuf("vcar%d" % l, [128, 256], BF16) for l in range(L)]
        ucar = [fw.sbuf("ucar%d" % l, [128, 8, 2], F32) for l in range(L)]
        Bhi = fw.sbuf("Bhi", [128, 16, 256], BF16)
        Blo = fw.sbuf("Blo", [128, 16, 256], BF16)
        diag4 = [fw.sbuf("diag4_%d" % i, [128, 4, 128], BF16) for i in range(2)]
        mrow = fw.sbuf("mrow", [1, 256], BF16)
        ring = [fw.sbuf("ring%d" % i, [128, 4096], BF16) for i in range(R)]
        ring_sem = [fw.new_sem() for _ in range(R)]
        TMPW = 400 if G > 512 else 512
        tmpf = [fw.sbuf("tmpf%d" % i, [128, TMPW], F32) for i in range(3)]
        sbt = [fw.sbuf("sbt0", [128, 4, 256], F32)]
        attb = fw.sbuf("attb", [128, 4, 1024], BF16)
        Pn = [T("Pn%d" % i, attb.ap[:, i, :].rearrange("p (h k) -> p h k", h=4)) for i in range(2)]
        PTs = [T("PTs%d" % i, attb.ap[:, 2 + i, :]) for i in range(2)]
        st = [fw.sbuf("st%d" % i, [128, 32], F32) for i in range(2)]
        ident = fw.sbuf("ident", [128, 128], BF16)
        identf = fw.sbuf("identf", [128, 128], F32)
        ones = fw.sbuf("ones", [128, 128], BF16)
        gT = fw.sbuf("gT", [128, L * 48], F32)
        gH = fw.sbuf("gH", [128, L * 48], F32)
        cwT = fw.sbuf("cwT", [128, L * 24], F32)
        sinkb = fw.sbuf("sinkb", [128, L * 16], F32)
        nsinkb = fw.sbuf("nsinkb", [128, L * 16], F32)
        sinkd = fw.sbuf("sinkd", [128, L], F32)
        nsinkd = fw.sbuf("nsinkd", [128, L], F32)
        flag = fw.sbuf("flag", [128, 1], F32)
        maskadd = fw.sbuf("maskadd", [128, 1], F32)
        epst = fw.sbuf("epst", [128, 1], F32)
        KcT = T("KcT", actT.ap[:, 16:16 + NKC, :].rearrange("p a b -> p (a b)")[:, 0:2 * NS * 128].rearrange("p (g b j) -> p g b j", g=2, b=NS))
        Vc = T("Vc", attb.ap[:, :, :].rearrange("p a b -> p (a b)")[:, 0:NS * 256].rearrange("p (b f) -> p b f", b=NS))
        stT = fw.sbuf("stT", [128, 8, 2, NS], F32)
        Bd = fw.sbuf("Bd", [128, 128], F32)
        es2 = ExitStack()
        setupA = T("setupA", es2.enter_context(nc.sbuf_tensor("sb_setupA", [32, 16 + 128 + 384 + 128], F32)))
        setupB = T("setupB", es2.enter_context(nc.sbuf_tensor("sb_setupB", [1, 384 + 128 + 16 + 128], F32)))
        Tt = T("Tt", es2.enter_context(nc.sbuf_tensor("sb_Tt", [16, 384], F32)))

        PSA = fw.psum("PSA", [128, 1024], F32)
        PSB = fw.psum("PSB", [128, 1024], F32)
        P4 = fw.psum("P4", [128, 512], F32)
        P5 = fw.psum("P5", [128, 512], F32)
        P6 = fw.psum("P6", [128, 512], F32)
        PTB = fw.psum("PTB", [128, 512], F32)
        PSX = [PSA, PSB]
        POX = [P4, P5]

        scrT = T("scr", None)
        ksT = T("ks", None)
        vsT = T("vs", None)

        sem_x = fw.new_sem()
        sem_c = fw.new_sem()
        sem_scr = fw.new_sem()
        sem_y = fw.new_sem()
        sem_o = [fw.new_sem() for _ in range(8)]
        sem_kc = fw.new_sem()
        sem_vc = fw.new_sem()
        sem_st = fw.new_sem()
        sem_v0 = fw.new_sem()
        sem_copy = fw.new_sem()

        plan = []

        def plan_ffn(wgu, wd, l):
            v = wgu[l].rearrange("(c p) n -> p c n", p=128)
            for u in range(NFC // 2):
                plan.append(([(v[:, :, u * 256:(u + 1) * 256], 0, 256), (v[:, :, DFF + u * 256:DFF + (u + 1) * 256], 256, 512)], [8, 512]))
            v2 = wd[l].rearrange("(c p) n -> p c n", p=128)
            for oc in range(8):
                plan.append(([(v2[:, :, oc * 128:(oc + 1) * 128], 0, 128)], [NFC, 128]))

        IN_UNITS = [0, 1, 6, 7, 8, 9, 10, 11, 12]

        def plan_mixer(l):
            v = win_d[l].rearrange("(c p) n -> p c n", p=128)
            for u in range(4):
                plan.append(([(v[:, :, 1024 + u * 256:1024 + (u + 1) * 256], 0, 256), (v[:, :, 2048 + u * 256:2048 + (u + 1) * 256], 256, 512)], [8, 512]))
            for u in IN_UNITS:
                plan.append(([(v[:, :, u * 512:(u + 1) * 512], 0, 512)], [8, 512]))
            for w in (wco_d, wao_d, wo_d):
                vv = w[l].rearrange("(c p) n -> p c n", p=128)
                for hf in range(2):
                    plan.append(([(vv[:, :, hf * 512:(hf + 1) * 512], 0, 512)], [8, 512]))

        for gi in range(NG):
            for l in range(L):
                plan_ffn(w1gu_d, w1d_d, l)
                plan_mixer(l)
                plan_ffn(w2gu_d, w2d_d, l)

        wstate = {"issued": 0, "used": 0}

        def wissue():
            i = wstate["issued"]
            srcs, vs = plan[i]
            slot = i % R
            ne = vs[0] * vs[1]
            dst = ring[slot][:, 0:ne].rearrange("p (a b) -> p a b", a=vs[0])
            for k_, (src, lo, hi) in enumerate(srcs):
                fw.dma(pool, dst[:, :, lo:hi], src, ring_sem[slot], writes=[(ring[slot], k_)] if len(srcs) > 1 else [ring[slot]])
            if len(srcs) > 1:
                r_ = Rec()
                _put(r_.w, (ring_sem[slot], ring_sem[slot].n))
                ring[slot].recs = {None: r_}
            wstate["issued"] = i + 1

        def wnext():
            i = wstate["used"]
            while wstate["issued"] < min(i + R, len(plan)):
                wissue()
            assert wstate["issued"] > i
            src, vs = plan[i]
            slot = i % R
            ne = vs[0] * vs[1]
            wstate["used"] = i + 1
            return ring[slot], ring[slot][:, 0:ne].rearrange("p (a b) -> p a b", a=vs[0])

        fw.dma(sp, xT[:, :, 0:G], xT_d.rearrange("(c p) n -> p c n", p=128)[:, :, 0:G], sem_x, writes=[xT])
        fw.dma(sp, gT[:], gT_d, sem_c, writes=[gT])
        fw.dma(sp, cwT[:], cwT_d, sem_c, writes=[cwT])
        fw.dma(sp, sinkb[:], sinkb_d, sem_c, writes=[sinkb])
        fw.dma(sp, sinkd[:], sinkd_d, sem_c, writes=[sinkd])
        fw.dma(sp, flag[:], flag_d, sem_c, writes=[flag])
        fw.dma(sp, setupA[:, 0:16], relb_d, sem_c, writes=[setupA])
        fw.dma(sp, setupA[:, 16:144], relbrep_d, sem_c, writes=[setupA])
        fw.dma(sp, setupA[:, 144:528], oh_d, sem_c, writes=[setupA])
        fw.dma(sp, setupA[:, 528:656], ohdec_d, sem_c, writes=[setupA])
        fw.dma(sp, setupB[:, 0:384], negm_d, sem_c, writes=[setupB])
        fw.dma(sp, setupB[:, 384:512], negmdec_d, sem_c, writes=[setupB])
        for t_ in (gT, cwT, sinkb, sinkd, flag, setupA, setupB):
            r_ = Rec()
            _put(r_.w, (sem_c, sem_c.n))
            t_.recs = {None: r_}

        fw.op(dve, lambda: V_.memset(ident[:], 0.0), writes=[ident])
        fw.op(pool, lambda: nc.gpsimd.affine_select(out=ident[:], in_=ident[:], compare_op=ALU.not_equal, fill=1.0,
                                                    base=0, pattern=[[-1, 128]], channel_multiplier=1),
              reads=[ident], writes=[ident])
        fw.op(dve, lambda: V_.tensor_copy(out=identf[:], in_=ident[:]), reads=[ident], writes=[identf])
        fw.op(dve, lambda: V_.memset(ones[:], 1.0), writes=[ones])
        fw.op(dve, lambda: V_.memset(epst[:], EPS), writes=[epst])
        fw.op(dve, lambda: V_.memset(setupB[:, 512:656], 1.0), writes=[setupB])
        fw.op(dve, lambda: V_.tensor_scalar(out=gH[:], in0=gT[:], scalar1=0.5, scalar2=None, op0=ALU.mult),
              reads=[gT], writes=[gH])
        fw.op(dve, lambda: V_.tensor_scalar(out=nsinkb[:], in0=sinkb[:], scalar1=-1.0, scalar2=None, op0=ALU.mult),
              reads=[sinkb], writes=[nsinkb])
        fw.op(dve, lambda: V_.tensor_scalar(out=nsinkd[:], in0=sinkd[:], scalar1=-1.0, scalar2=None, op0=ALU.mult),
              reads=[sinkd], writes=[nsinkd])
        fw.op(dve, lambda: V_.tensor_scalar(out=maskadd[:], in0=flag[:], scalar1=-NEGBIG, scalar2=NEGBIG,
                                            op0=ALU.mult, op1=ALU.add), reads=[flag], writes=[maskadd])
        for l in range(L):
            fw.op(dve, lambda l=l: V_.memset(kcar[l][:], 0.0), writes=[kcar[l]])
            fw.op(dve, lambda l=l: V_.memset(vcar[l][:], 0.0), writes=[vcar[l]])
            fw.op(dve, lambda l=l: V_.memset(ucar[l][:], 0.0), writes=[ucar[l]])
        for i in range(2):
            fw.op(dve, lambda i=i: V_.memset(kTz[i][:], 0.0), writes=[kTz[i]])
            fw.op(dve, lambda i=i: V_.memset(qsz[i][:], 0.0), writes=[qsz[i]])
        fw.op(pe, lambda: P_.matmul(P6[0:16, 0:384], setupA[:, 0:16], setupA[:, 144:528], start=True, stop=False),
              reads=[setupA], writes=[P6], inc=False)
        fw.op(pe, lambda: P_.matmul(P6[0:16, 0:384], setupB[:, 512:528], setupB[:, 0:384], start=False, stop=True),
              reads=[setupB], writes=[P6])
        fw.op(act, lambda: A_.activation(out=Tt[:], in_=P6[0:16, 0:384], func=AF.Copy), reads=[P6], writes=[Tt])
        fw.op(pe, lambda: P_.matmul(P5[:, 0:128], setupA[:, 16:144], setupA[:, 528:656], start=True, stop=True),
              reads=[setupA], writes=[P5])
        fw.op(act, lambda: A_.activation(out=Bd[:], in_=P5[:, 0:128], func=AF.Copy), reads=[P5], writes=[Bd])
        dst = bass.AP(scr, 0, [[128 * 385, 16], [385, 128], [1, 384]])
        fw.dma(sp, dst, Tt[:, :].unsqueeze(1).to_broadcast([16, 128, 384]), sem_scr, reads=[Tt], writes=[scrT])
        src = bass.AP(scr, 127, [[384, 128], [128 * 385, 16], [1, 256]])
        if 8 * (2 + GX) >= 4096:
            Bt = T("Bt_tmp", yT.ap[:, :, :].rearrange("p a b -> p (a b)")[:, 0:4096].rearrange("p (s k) -> p s k", s=16))
        else:
            Bt = T("Bt_tmp", es2.enter_context(nc.sbuf_tensor("sb_Bt_tmp", [128, 16, 256], F32)))
        fw.dma(sp, Bt[:], src, sem_scr, reads=[scrT], writes=[Bt])
        def bias_conv():
            fw.op(dve, lambda: V_.tensor_scalar(out=Bhi[:], in0=Bt[:], scalar1=8.0, scalar2=None, op0=ALU.mult), reads=[Bt], writes=[Bhi])
            fw.op(dve, lambda: V_.scalar_tensor_tensor(out=Blo[:], in0=Bt[:], scalar=8.0, in1=Bhi[:], op0=ALU.mult, op1=ALU.subtract),
                  reads=[Bt, Bhi], writes=[Blo])
            fw.inherit(yT, [Bt])
        late_hooks = [bias_conv]
        for l in range(L):
            fw.dma(sp, ks_o[l, :, 0:127, :], kc_d[l, :, 1:128, :], sem_copy, writes=[(ksT, ("c", l))])
            fw.dma(sp, vs_o[l, :, 0:127, :], vc_d[l, :, 1:128, :], sem_copy, writes=[(vsT, ("c", l))])
        fw.op(dve, lambda: V_.memset(mrow[:], 0.0), writes=[mrow])
        fw.op(dve, lambda: V_.tensor_scalar(out=mrow[:, 0:128], in0=maskadd[0:1, 0:1].to_broadcast([1, 128]), scalar1=8.0, scalar2=None, op0=ALU.mult),
              reads=[maskadd, mrow], writes=[mrow])
        es2.close()
        kvn = fw.sbuf("kvn", [NS, 512], F32)
        STd = T("STd", sbt[0].ap[:, 0, :])
        Sd = T("Sd", sbt[0].ap[:, 1, :].rearrange("p (h k) -> p h k", h=2))
        Pd = T("Pd", sbt[0].ap[:, 2, :].rearrange("p (h k) -> p h k", h=2))
        PTd = fw.sbuf("PTd", [128, NS * 16], BF16)
        kp_sb = fw.sbuf("kp_sb", [128, 2, 128], F32)
        vp_sb = fw.sbuf("vp_sb", [128, 256], F32)
        for t_ in (kvn, PTd, kp_sb, vp_sb):
            fw.inherit(t_, [setupA, setupB, Tt])

        ck = {"n": 0}

        def checkpoint(name=None):
            if name is not None:
                if isinstance(stop_after, str) and stop_after == "%d:%d:%s" % (ck.get("gi", 0), ck.get("l", 0), name):
                    raise _Stop()
                return
            ck["n"] += 1
            if isinstance(stop_after, int) and ck["n"] > stop_after:
                raise _Stop()

        cur = {"skip": 0}

        def set_skip(gi, l, late):
            cur["skip"] = (128 * l + (128 if late else 0)) if (gi == 0 and G > 384) else 0

        def tiles_base(gi):
            gx = GX if gi == NG - 1 else G
            if G <= 512 and gx <= 512:
                return [(0, gx)]
            nt = (G + 383) // 384
            ts = []
            for t in range(nt):
                c0 = t * 384
                c1 = min(G, c0 + 384)
                if t == nt - 1:
                    c1 = gx
                ts.append((c0, c1 - c0))
            return ts

        def tiles_of(gi):
            ts = tiles_base(gi)
            sk = cur["skip"] if gi == 0 else 0
            if sk:
                c0, n = ts[0]
                assert sk < c0 + n
                ts[0] = (sk, c0 + n - sk)
            return ts

        def tile_of_block(gi, b):
            ts = tiles_base(gi)
            for t, (c0, n) in enumerate(ts):
                if c0 <= b * 128 < c0 + n:
                    return t
            raise AssertionError

        def gcol(l, i, c):
            k = (l * 6 + i) * 8 + c
            return k

        def norm_stats(ts_t, t, c0, n):
            for c in range(8):
                fw.op(pe, lambda c=c: P_.matmul(P6[:, 0:n], ones[:], bfB[:, c, c0:c0 + n], start=(c == 0), stop=(c == 7)),
                      reads=[ones, (bfB, (c, t))], writes=[P6], inc=(c == 7))
            fw.op(act, lambda: A_.activation(out=rstd[:, c0:c0 + n], in_=P6[:, 0:n], func=AF.Sqrt, scale=1.0 / D, bias=epst[:]),
                  reads=[P6, epst], writes=[(rstd, t)])
            fw.op(dve, lambda: V_.reciprocal(out=rstd[:, c0:c0 + n], in_=rstd[:, c0:c0 + n]), reads=[(rstd, t)], writes=[(rstd, t)])

        sqstate = {"ready": False}

        def pre_norm(gi, l, i):
            have_sq = sqstate["ready"]
            sqstate["ready"] = False
            for t, (c0, n) in enumerate(tiles_of(gi)):
                if not have_sq:
                    fw.op(act, lambda: A_.activation(out=bfB[:, :, c0:c0 + n], in_=xT[:, :, c0:c0 + n], func=AF.Square),
                          reads=[(xT, (c, t)) for c in range(8)], writes=[(bfB, (c, t)) for c in range(8)])
                norm_stats(None, t, c0, n)
                for c in range(8):
                    k = gcol(l, i, c)
                    fw.op(dve, lambda c=c, k=k: V_.scalar_tensor_tensor(out=hT[:, c, c0:c0 + n], in0=xT[:, c, c0:c0 + n],
                                                                       scalar=gT[:, k:k + 1], in1=rstd[:, c0:c0 + n],
                                                                       op0=ALU.mult, op1=ALU.mult),
                          reads=[(xT, (c, t)), gT, (rstd, t)], writes=[(hT, (c, t))])

        def post_norm(gi, l, i, half, presquare=False):
            gsrc = gH if half else gT
            sqstate["ready"] = presquare
            for t, (c0, n) in enumerate(tiles_of(gi)):
                norm_stats(None, t, c0, n)
                checkpoint("post%d_stats" % i)
                for c in range(8):
                    k = gcol(l, i, c)
                    checkpoint("post%d_c%d" % (i, c))
                    fw.op(dve, lambda c=c, k=k: V_.scalar_tensor_tensor(out=yT[:, c, 2 + c0:2 + c0 + n], in0=yT[:, c, 2 + c0:2 + c0 + n],
                                                                       scalar=gsrc[:, k:k + 1], in1=rstd[:, c0:c0 + n],
                                                                       op0=ALU.mult, op1=ALU.mult),
                          reads=[(yT, (c, t)), gsrc, (rstd, t)], writes=[(yT, (c, t))])
                    if c == 0:
                        checkpoint("post%d_stt" % i)
                    fw.op(dve, lambda c=c: V_.tensor_tensor(out=xT[:, c, c0:c0 + n], in0=xT[:, c, c0:c0 + n],
                                                            in1=yT[:, c, 2 + c0:2 + c0 + n], op=ALU.add),
                          reads=[(xT, (c, t)), (yT, (c, t))], writes=[(xT, (c, t))])
                    if presquare:
                        fw.op(act, lambda c=c: A_.activation(out=bfB[:, c, c0:c0 + n], in_=xT[:, c, c0:c0 + n], func=AF.Square),
                              reads=[(xT, (c, t))], writes=[(bfB, (c, t))])

        cnt = {"ps": 0, "po": 0, "tmp": 0}

        def next_ps():
            cnt["ps"] += 1
            return PSX[cnt["ps"] % 2]

        def next_ps4():
            cnt["ps"] += 1
            k = cnt["ps"] % 4
            return PSX[k % 2], ("h", k // 2), (k // 2) * 512

        def next_po():
            cnt["po"] += 1
            return POX[cnt["po"] % 2]

        def next_tmp():
            cnt["tmp"] += 1
            return tmpf[cnt["tmp"] % 3]

        def proj(ps_ap, ps_t, wslot, wv, j0, src, t, c0, n, last_inc=True):
            for kc in range(8):
                fw.op(pe, lambda kc=kc: P_.matmul(ps_ap, wv[:, kc, j0:j0 + 128], src[:, kc, c0:c0 + n], start=(kc == 0), stop=(kc == 7)),
                      reads=[wslot, (src, (kc, t))], writes=[ps_t], inc=(kc == 7 and last_inc))

        def dual_proj(gi, w, nj, func, dst, dst_off, dst_chunk0):
            sA, vA = w
            for t, (c0, n) in enumerate(tiles_of(gi)):
                for j in range(nj):
                    ps = next_ps()
                    proj(ps[:, 0:n], ps, sA, vA, j * 128, hT, t, c0, n, last_inc=False)
                    proj(ps[:, 512:512 + n], ps, sA, vA, 256 + j * 128, hT, t, c0, n)
                    tm = next_tmp()
                    fw.op(act, lambda: A_.activation(out=tm[:, 0:n], in_=ps[:, 0:n], func=func), reads=[ps], writes=[tm])
                    cch = dst_chunk0 + j
                    fw.op(dve, lambda: V_.tensor_tensor(out=dst[:, cch, dst_off + c0:dst_off + c0 + n], in0=tm[:, 0:n],
                                                        in1=ps[:, 512:512 + n], op=ALU.mult),
                          reads=[tm, ps], writes=[(dst, (cch, t))])

        def ffn(gi, l, ipre, ipost, presquare=False):
            ck["gi"], ck["l"] = gi, l
            checkpoint("ffn%d_start" % ipre)
            pre_norm(gi, l, ipre)
            checkpoint("ffn%d_pre" % ipre)
            for u in range(NFC // 2):
                dual_proj(gi, wnext(), 2, AF.Silu, actT, 0, 2 * u)
            checkpoint("ffn%d_dual" % ipre)
            while late_hooks:
                late_hooks.pop()()
            for oc in range(8):
                ws, wv = wnext()
                for t, (c0, n) in enumerate(tiles_of(gi)):
                    po = next_po()
                    for fc in range(NFC):
                        fw.op(pe, lambda fc=fc: P_.matmul(po[:, 0:n], wv[:, fc, :], actT[:, fc, c0:c0 + n], start=(fc == 0), stop=(fc == NFC - 1)),
                              reads=[ws, (actT, (fc, t))], writes=[po], inc=(fc == NFC - 1))
                    fw.op(act, lambda: A_.activation(out=yT[:, oc, 2 + c0:2 + c0 + n], in_=po[:, 0:n], func=AF.Copy),
                          reads=[po], writes=[(yT, (oc, t))])
                    fw.op(act, lambda: A_.activation(out=bfB[:, oc, c0:c0 + n], in_=po[:, 0:n], func=AF.Square),
                          reads=[po], writes=[(bfB, (oc, t))])
            checkpoint("ffn%d_ph2" % ipre)
            post_norm(gi, l, ipost, True, presquare)

        def attn_stageA(gi, l, b, gp, hb, k, masked):
            c0q = b * 128
            tq = tile_of_block(gi, b)
            ps = PSX[k % 2]
            st_, pfb = st[k % 2], Pn[k % 2]
            ch0 = 4 * gp + 2 * hb
            s0 = 2 * ch0
            for i in range(4):
                cc, hh = divmod(i, 2)
                o_ = ps[:, i * 256:(i + 1) * 256]
                fw.op(pe, lambda cc=cc, hh=hh, o_=o_: P_.matmul(o_, bfB[:, ch0 + cc, c0q:c0q + 128], kTz[hh][:, gp, c0q:c0q + 256], start=True, stop=False),
                      reads=[(bfB, (ch0 + cc, tq)), kTz[hh]], writes=[ps], inc=False)
                fw.op(pe, lambda i=i, o_=o_: P_.matmul(o_, ident[:], Bhi[:, s0 + i, :], start=False, stop=False),
                      reads=[ident, Bhi], writes=[ps], inc=False)
                if masked:
                    fw.op(pe, lambda o_=o_: P_.matmul(o_, ones[0:1, :], mrow[:, :], start=False, stop=False),
                          reads=[ones, mrow], writes=[ps], inc=False)
                fw.op(pe, lambda i=i, o_=o_: P_.matmul(o_, ident[:], Blo[:, s0 + i, :], start=False, stop=True),
                      reads=[ident, Blo], writes=[ps], inc=(i == 3))
            fw.op(dve, lambda: V_.reduce_max(out=st_[:, 0:4], in_=ps[:, :].rearrange("p (h k) -> p h k", h=4), axis=AX.X), reads=[ps], writes=[st_])
            fw.op(dve, lambda: V_.scalar_tensor_tensor(out=st_[:, 4:8], in0=st_[:, 0:4], scalar=-0.125,
                                                       in1=nsinkb[:, l * 16 + s0:l * 16 + s0 + 4], op0=ALU.mult, op1=ALU.min),
                  reads=[st_, nsinkb], writes=[st_])
            fw.op(dve, lambda: V_.tensor_tensor(out=st_[:, 8:12], in0=st_[:, 4:8], in1=sinkb[:, l * 16 + s0:l * 16 + s0 + 4], op=ALU.add),
                  reads=[st_, sinkb], writes=[st_])
            for i in range(4):
                fw.op(act, lambda i=i: A_.activation(out=pfb[:, i, :], in_=ps[:, i * 256:(i + 1) * 256], func=AF.Exp, bias=st_[:, 4 + i:5 + i], scale=0.125,
                                                     accum_out=st_[:, 12 + i:13 + i]),
                      reads=[ps, (st_, "nm")], writes=[(pfb, i), (st_, ("rs", i))])
            fw.op(act, lambda: A_.activation(out=st_[:, 16:20], in_=st_[:, 8:12], func=AF.Exp), reads=[st_], writes=[st_])

        def attn_stageB(gi, l, b, gp, hb, k):
            c0q = b * 128
            tq = tile_of_block(gi, b)
            st_, pfb, pts, dg = st[k % 2], Pn[k % 2], PTs[k % 2], diag4[k % 2]
            Vt = Vl[l]
            ch0 = 4 * gp + 2 * hb
            fw.op(dve, lambda: V_.tensor_tensor(out=st_[:, 20:24], in0=st_[:, 12:16], in1=st_[:, 16:20], op=ALU.add), reads=[st_], writes=[st_])
            fw.op(dve, lambda: V_.reciprocal(out=st_[:, 24:28], in_=st_[:, 20:24]), reads=[st_], writes=[st_])
            fw.op(dve, lambda: V_.tensor_tensor(out=dg[:], in0=ident[:, :].unsqueeze(1).to_broadcast([128, 4, 128]),
                                                in1=st_[:, 24:28].unsqueeze(2).to_broadcast([128, 4, 128]), op=ALU.mult),
                  reads=[ident, st_], writes=[dg])
            for i in range(4):
                ptt = PTB if i < 2 else P6
                for kb in range(2):
                    j = (i % 2) * 2 + kb
                    fw.op(pe, lambda i=i, kb=kb, j=j, ptt=ptt: P_.matmul(ptt[:, j * 128:(j + 1) * 128], pfb[:, i, kb * 128:(kb + 1) * 128], dg[:, i, :],
                                                                         start=True, stop=True),
                          reads=[pfb, dg], writes=[ptt], inc=(j == 3))
            fw.op(dve, lambda: V_.tensor_copy(out=pts[:, 0:512], in_=PTB[:, :]), reads=[PTB], writes=[(pts, 0)])
            fw.op(act, lambda: A_.activation(out=pts[:, 512:1024], in_=P6[:, :], func=AF.Copy), reads=[P6], writes=[(pts, 1)])
            po = POX[k % 2]
            for i in range(4):
                for kb in range(2):
                    j = i * 2 + kb
                    fw.op(pe, lambda i=i, kb=kb, j=j: P_.matmul(po[:, i * 128:(i + 1) * 128],
                                                                 Vt[:, b + kb, gp * 128:(gp + 1) * 128], pts[:, j * 128:(j + 1) * 128],
                                                                 start=(kb == 0), stop=(kb == 1)),
                          reads=[Vt, pts], writes=[po], inc=(j == 7))
            for hh in range(2):
                fw.op(act, lambda hh=hh: A_.activation(out=bfA[hh * 64:(hh + 1) * 64, ch0:ch0 + 2, c0q:c0q + 128],
                                                       in_=po[hh * 64:(hh + 1) * 64, :].rearrange("p (c h q) -> p c h q", c=2, h=2)[:, :, hh, :], func=AF.Copy),
                      reads=[po], writes=[(bfA, (ch0 + c, tq)) for c in range(2)])

        def attention_prompt(gi, l):
            first_valid = divmod(2, NB)
            b_lo = (cur["skip"] // 128) if gi == 0 else 0
            batches = [(b, gp, hb) for b in range(b_lo, NB) for gp in range(2) for hb in range(2)]
            prev = None
            for k, (b, gp, hb) in enumerate(batches):
                attn_stageA(gi, l, b, gp, hb, k, masked=((gi, b) == first_valid))
                if prev is not None:
                    attn_stageB(gi, l, *prev)
                prev = (b, gp, hb, k)
            attn_stageB(gi, l, *prev)

        def attention_decode(l):
            tl = len(tiles_of(NG - 1)) - 1
            for hh in range(2):
                fw.op(dve, lambda hh=hh: V_.tensor_copy(out=KcT[hh * 64:(hh + 1) * 64, :, :, 0], in_=kTz[hh][hh * 64:(hh + 1) * 64, :, 128 + G:128 + GX]),
                      reads=[kTz[hh], KcT], writes=[KcT])
                fw.op(dve, lambda hh=hh: V_.tensor_copy(out=qsz[hh][hh * 64:(hh + 1) * 64, :, :], in_=bfB[hh * 64:(hh + 1) * 64, :, G:GX]),
                      reads=[(bfB, (c, tl)) for c in range(8)], writes=[qsz[hh]])
            n = 0
            for b in range(NS):
                for gp in range(2):
                    for gh in range(2):
                        n += 1
                        fw.op(pe, lambda b=b, gp=gp, gh=gh: P_.matmul(P6[:, b * 16 + 8 * gp + gh:b * 16 + 8 * gp + 8:2],
                                                                      KcT[:, gp, b, :],
                                                                      qsz[gh][:, 4 * gp:4 * gp + 4, b], start=True, stop=True),
                              reads=[KcT, qsz[gh]], writes=[P6], inc=(n == NS * 4))
            fw.op(act, lambda: A_.activation(out=STd[:], in_=P6[:, 0:NS * 16], func=AF.Copy), reads=[P6], writes=[STd])
            nh = (NS * 16) // 128
            for h in range(nh):
                fw.op(pe, lambda h=h: P_.transpose(P5[:, h * 128:(h + 1) * 128], STd[:, h * 128:(h + 1) * 128], identf[:]),
                      reads=[STd, identf], writes=[P5], inc=(h == nh - 1))
            st_ = st[0]
            fw.op(dve, lambda: V_.scalar_tensor_tensor(out=Sd[:], in0=P5[:, 0:nh * 128].rearrange("p (h k) -> p h k", h=nh), scalar=0.125,
                                                       in1=Bd[:, :].unsqueeze(1).to_broadcast([128, nh, 128]), op0=ALU.mult, op1=ALU.add),
                  reads=[P5, Bd], writes=[Sd])
            fw.op(dve, lambda: V_.reduce_max(out=st_[:, 0:nh], in_=Sd[:], axis=AX.X), reads=[Sd], writes=[st_])
            fw.op(dve, lambda: V_.tensor_scalar(out=st_[:, 4:4 + nh], in0=st_[:, 0:nh], scalar1=-1.0, scalar2=nsinkd[:, l:l + 1], op0=ALU.mult, op1=ALU.min),
                  reads=[st_, nsinkd], writes=[st_])
            fw.op(dve, lambda: V_.tensor_scalar(out=st_[:, 8:8 + nh], in0=st_[:, 4:4 + nh], scalar1=sinkd[:, l:l + 1], scalar2=None, op0=ALU.add),
                  reads=[st_, sinkd], writes=[st_])
            for h in range(nh):
                fw.op(act, lambda h=h: A_.activation(out=Pd[:, h, :], in_=Sd[:, h, :], func=AF.Exp, bias=st_[:, 4 + h:5 + h], scale=1.0,
                                                     accum_out=st_[:, 12 + h:13 + h]), reads=[Sd, st_], writes=[Pd, st_])
            fw.op(act, lambda: A_.activation(out=st_[:, 16:16 + nh], in_=st_[:, 8:8 + nh], func=AF.Exp), reads=[st_], writes=[st_])
            fw.op(dve, lambda: V_.tensor_tensor(out=st_[:, 20:20 + nh], in0=st_[:, 12:12 + nh], in1=st_[:, 16:16 + nh], op=ALU.add), reads=[st_], writes=[st_])
            fw.op(dve, lambda: V_.reciprocal(out=st_[:, 24:24 + nh], in_=st_[:, 20:20 + nh]), reads=[st_], writes=[st_])
            fw.op(dve, lambda: V_.tensor_tensor(out=Pd[:], in0=Pd[:], in1=st_[:, 24:24 + nh].unsqueeze(2).to_broadcast([128, nh, 128]), op=ALU.mult),
                  reads=[Pd, st_], writes=[Pd])
            for h in range(nh):
                fw.op(pe, lambda h=h: P_.transpose(P4[:, h * 128:(h + 1) * 128], Pd[:, h, :], identf[:]),
                      reads=[Pd, identf], writes=[P4], inc=(h == nh - 1))
            fw.op(act, lambda: A_.activation(out=PTd[:], in_=P4[:, 0:nh * 128], func=AF.Copy), reads=[P4], writes=[PTd])
            n = 0
            for b in range(NS):
                for gp in range(2):
                    for gh in range(2):
                        n += 1
                        base = (gh * 8 + gp * 4) * NS
                        fw.op(pe, lambda b=b, gp=gp, gh=gh, base=base: P_.matmul(P5[:, base + b:base + 4 * NS + b:NS],
                                                                                 Vc[:, b, gp * 128:(gp + 1) * 128],
                                                                                 PTd[:, b * 16 + 8 * gp + gh:b * 16 + 8 * gp + 8:2], start=True, stop=True),
                              reads=[Vc, PTd], writes=[P5], inc=(n == NS * 4))
            for gh in range(2):
                fw.op(act, lambda gh=gh: A_.activation(out=bfA[gh * 64:(gh + 1) * 64, :, G:GX],
                                                       in_=P5[gh * 64:(gh + 1) * 64, gh * 8 * NS:(gh + 1) * 8 * NS].rearrange("p (c b) -> p c b", c=8), func=AF.Copy),
                      reads=[P5], writes=[(bfA, (c, tl)) for c in range(8)])

        def sq_proj(gi, wlist, src, epilogue):
            for hf in range(2):
                ws, wv = wnext()
                for j in range(4):
                    oc = hf * 4 + j
                    checkpoint()
                    for t, (c0, n) in enumerate(tiles_of(gi)):
                        po = next_po()
                        proj(po[:, 0:n], po, ws, wv, j * 128, src, t, c0, n)
                        epilogue(oc, t, c0, n, po)

        def mixer(gi, l):
            last = (gi == NG - 1)
            Vt = Vl[l]
            set_skip(gi, l, False)
            ts = tiles_of(gi)
            tl = len(ts) - 1
            sigA = T("sigA", actT.ap)
            sigB = T("sigB", actT.ap)
            fw.inherit(sigA, [actT])
            fw.inherit(sigB, [actT])
            ck["gi"], ck["l"] = gi, l
            pre_norm(gi, l, 2)
            checkpoint("mix_pre")
            if last:
                fw.inherit(KcT, [actT, sigA, sigB])
                fw.dma(pool, KcT[:], kcT_d[l], sem_kc, writes=[KcT])
                fw.dma(sp, stT[:], stT_d[l], sem_st, writes=[stT])
            checkpoint("mix_loads")
            fw.op(dve, lambda: V_.tensor_copy(out=yT[:, :, 0:2], in_=ucar[l][:]), reads=[ucar[l]], writes=[(yT, (c, "car")) for c in range(8)])
            for hh in range(2):
                fw.op(dve, lambda hh=hh: V_.tensor_copy(out=kTz[hh][hh * 64:(hh + 1) * 64, :, 0:128], in_=kcar[l][hh * 64:(hh + 1) * 64, :, :]),
                      reads=[kcar[l]], writes=[kTz[hh]])
            fw.op(dve, lambda: V_.tensor_copy(out=Vt[:, 0, :], in_=vcar[l][:]), reads=[vcar[l]], writes=[Vt])
            checkpoint("mix_carry")
            for uu in range(4):
                dual_proj(gi, wnext(), 2, AF.Copy, yT, 2, 2 * uu)
                checkpoint("mix_u%d" % uu)
            checkpoint()
            set_skip(gi, l, True)
            ts = tiles_of(gi)
            for uu in range(2):
                ws, wv = wnext()
                for j in range(4):
                    c = 4 * uu + j
                    for t, (c0, n) in enumerate(ts):
                        npr = (G - c0) if (last and t == tl) else n
                        tm = next_tmp()
                        rd = [(yT, (c, t)), cwT] + ([(yT, (c, t - 1))] if t > 0 else [(yT, (c, "car"))])
                        w0, w1, w2 = [cwT[:, (l * 3 + i) * 8 + c:(l * 3 + i) * 8 + c + 1] for i in range(3)]
                        fw.op(dve, lambda: V_.tensor_scalar(out=tm[:, 0:npr], in0=yT[:, c, c0:c0 + npr], scalar1=w0, scalar2=None, op0=ALU.mult),
                              reads=rd, writes=[tm])
                        fw.op(dve, lambda: V_.scalar_tensor_tensor(out=tm[:, 0:npr], in0=yT[:, c, c0 + 1:c0 + 1 + npr], scalar=w1, in1=tm[:, 0:npr],
                                                                   op0=ALU.mult, op1=ALU.add), reads=rd + [tm], writes=[tm])
                        fw.op(dve, lambda: V_.scalar_tensor_tensor(out=tm[:, 0:npr], in0=yT[:, c, c0 + 2:c0 + 2 + npr], scalar=w2, in1=tm[:, 0:npr],
                                                                   op0=ALU.mult, op1=ALU.add), reads=rd + [tm], writes=[tm])
                        if last and t == tl:
                            fw.op(dve, lambda: V_.tensor_scalar(out=tm[:, npr:n], in0=stT[:, c, 0, :], scalar1=w0, scalar2=None, op0=ALU.mult),
                                  reads=[stT, cwT, tm], writes=[tm])
                            fw.op(dve, lambda: V_.scalar_tensor_tensor(out=tm[:, npr:n], in0=stT[:, c, 1, :], scalar=w1, in1=tm[:, npr:n],
                                                                       op0=ALU.mult, op1=ALU.add), reads=[stT, cwT, tm], writes=[tm])
                            fw.op(dve, lambda: V_.scalar_tensor_tensor(out=tm[:, npr:n], in0=yT[:, c, 2 + G:2 + GX], scalar=w2, in1=tm[:, npr:n],
                                                                       op0=ALU.mult, op1=ALU.add), reads=rd + [tm], writes=[tm])
                        ps, pk, po_ = next_ps4()
                        proj(ps[:, po_:po_ + n], (ps, pk), ws, wv, j * 128, hT, t, c0, n)
                        fw.op(dve, lambda: V_.tensor_tensor(out=bfA[:, c, c0:c0 + n], in0=tm[:, 0:n], in1=ps[:, po_:po_ + n], op=ALU.mult),
                              reads=[tm, (ps, pk)], writes=[(bfA, (c, t))])
            checkpoint()
            fw.op(dve, lambda: V_.tensor_copy(out=ucar[l][:], in_=yT[:, :, G:G + 2]), reads=[(yT, (c, tile_of_block(gi, NB - 1))) for c in range(8)], writes=[ucar[l]])
            if last:
                fw.dma(sp, convp_o[l], ucar[l][:], sem_o[0], reads=[ucar[l]])
                fw.dma(sp, convs_o[l][:, :, 0, :], stT[:, :, 1, :], sem_o[1], reads=[stT])
                fw.dma(sp, convs_o[l][:, :, 1, :], yT[:, :, 2 + G:2 + GX], sem_o[6], reads=[(yT, (c, tl)) for c in range(8)])
            checkpoint("mix_cout")
            for uu in range(2):
                ws, wv = wnext()
                for j in range(4):
                    c = 4 * uu + j
                    for t, (c0, n) in enumerate(ts):
                        ps, pk, po_ = next_ps4()
                        proj(ps[:, po_:po_ + n], (ps, pk), ws, wv, j * 128, hT, t, c0, n)
                        fw.op(act, lambda: A_.activation(out=bfB[:, c, c0:c0 + n], in_=ps[:, po_:po_ + n], func=AF.Copy), reads=[(ps, pk)], writes=[(bfB, (c, t))])
            checkpoint("mix_q")
            set_skip(gi, l, False)
            ts = tiles_of(gi)
            ws, wv = wnext()
            for gp in range(2):
                for t, (c0, n) in enumerate(ts):
                    ps, pk, po_ = next_ps4()
                    proj(ps[:, po_:po_ + n], (ps, pk), ws, wv, gp * 128, hT, t, c0, n)
                    for hh in range(2):
                        fw.op(act, lambda hh=hh: A_.activation(out=kTz[hh][hh * 64:(hh + 1) * 64, gp, 128 + c0:128 + c0 + n], in_=ps[hh * 64:(hh + 1) * 64, po_:po_ + n], func=AF.Copy),
                              reads=[(ps, pk)], writes=[kTz[hh]])
                    if last and t == tl:
                        o0 = (NB - 1) * 128 - c0
                        fw.op(act, lambda: A_.activation(out=kp_sb[:, gp, :], in_=ps[:, po_ + o0:po_ + o0 + 128], func=AF.Copy), reads=[(ps, pk)], writes=[kp_sb])
            checkpoint("mix_k")
            for b in range(NB):
                tb = tile_of_block(gi, b)
                ps = next_ps()
                for kc in range(8):
                    fw.op(pe, lambda kc=kc: P_.matmul(ps[:, 0:256], hT[:, kc, b * 128:(b + 1) * 128], wv[:, kc, 256:512], start=(kc == 0), stop=(kc == 7)),
                          reads=[ws, (hT, (kc, tb))], writes=[ps], inc=(kc == 7))
                fw.op(act, lambda: A_.activation(out=Vt[:, b + 1, :], in_=ps[:, 0:256], func=AF.Copy), reads=[ps], writes=[Vt])
                if last and b == NB - 1:
                    fw.op(act, lambda: A_.activation(out=vp_sb[:], in_=ps[:, 0:256], func=AF.Copy), reads=[ps], writes=[vp_sb])
            checkpoint("mix_v")
            if last:
                fw.dma(sp, kpT_o[l], kp_sb[:], sem_o[2], reads=[kp_sb])
                fw.dma(sp, vp_o[l], vp_sb[:], sem_o[3], reads=[vp_sb])
                ps = next_ps()
                for kc in range(8):
                    fw.op(pe, lambda kc=kc: P_.matmul(ps[0:NS, 0:512], hT[:, kc, G:GX], wv[:, kc, 0:512], start=(kc == 0), stop=(kc == 7)),
                          reads=[ws, (hT, (kc, tl))], writes=[ps], inc=(kc == 7))
                checkpoint("mix_kpvp")
                fw.op(act, lambda: A_.activation(out=kvn[:], in_=ps[0:NS, 0:512], func=AF.Copy), reads=[ps], writes=[kvn])
                checkpoint("mix_kvn")
                fw.dma(sp, ks_o[l, :, 127, :], kvn[:, 0:256], sem_o[4], reads=[kvn], writes=[(ksT, ("n", l))])
                fw.dma(sp, vs_o[l, :, 127, :], kvn[:, 256:512], sem_o[5], reads=[kvn], writes=[(vsT, ("n", l))])
            checkpoint()
            set_skip(gi, l, True)
            ts = tiles_of(gi)
            for which, dstv in ((0, sigA), (1, sigB)):
                for uu in range(2):
                    ws, wv = wnext()
                    for j in range(4):
                        c = 4 * uu + j
                        for t, (c0, n) in enumerate(ts):
                            ps, pk, po_ = next_ps4()
                            proj(ps[:, po_:po_ + n], (ps, pk), ws, wv, j * 128, hT, t, c0, n)
                            fw.op(act, lambda: A_.activation(out=actT[:, 8 * which + c, c0:c0 + n], in_=ps[:, po_:po_ + n], func=AF.Sigmoid),
                                  reads=[(ps, pk)], writes=[(dstv, (c, t))])
            checkpoint()
            wl = None

            def ep_co(oc, t, c0, n, po):
                fw.op(dve, lambda: V_.tensor_tensor(out=actT[:, oc, c0:c0 + n], in0=actT[:, oc, c0:c0 + n], in1=po[:, 0:n], op=ALU.mult),
                      reads=[(sigA, (oc, t)), po], writes=[(sigA, (oc, t))])
            sq_proj(gi, wl, bfA, ep_co)
            checkpoint()
            attention_prompt(gi, l)
            checkpoint()
            if last:
                fw.inherit(Vc, Pn + PTs)
                fw.dma(pool, Vc[:], vc_d[l].rearrange("b j f -> j b f"), sem_vc, writes=[Vc])
                vsrc = vs_o[l, :, 127:128, :].rearrange("b o f -> o b f")
                fw.dma(pool, Vc[0:1, :, :], vsrc, sem_v0, reads=[(vsT, ("n", l))], writes=[Vc])
                for t_ in (STd, Sd, Pd):
                    fw.inherit(t_, [sbt[0]])
                attention_decode(l)
                for t_ in Pn + PTs:
                    fw.inherit(t_, [Vc])
                fw.inherit(sbt[0], [sbt[0], STd, Sd, Pd])
            if not last:
                for hh in range(2):
                    fw.op(dve, lambda hh=hh: V_.tensor_copy(out=kcar[l][hh * 64:(hh + 1) * 64, :, :], in_=kTz[hh][hh * 64:(hh + 1) * 64, :, G:G + 128]),
                          reads=[kTz[hh]], writes=[kcar[l]])
                fw.op(dve, lambda: V_.tensor_copy(out=vcar[l][:], in_=Vt[:, NB, :]), reads=[Vt], writes=[vcar[l]])
            wl = None

            def ep_ao(oc, t, c0, n, po):
                fw.op(dve, lambda: V_.tensor_tensor(out=actT[:, 8 + oc, c0:c0 + n], in0=actT[:, 8 + oc, c0:c0 + n], in1=po[:, 0:n], op=ALU.mult),
                      reads=[(sigB, (oc, t)), po], writes=[(sigB, (oc, t))])
                fw.op(dve, lambda: V_.tensor_tensor(out=hT[:, oc, c0:c0 + n], in0=actT[:, 8 + oc, c0:c0 + n], in1=actT[:, oc, c0:c0 + n], op=ALU.add),
                      reads=[(sigB, (oc, t)), (sigA, (oc, t))], writes=[(hT, (oc, t))])
            sq_proj(gi, wl, bfA, ep_ao)
            checkpoint()
            wl = None

            def ep_wo(oc, t, c0, n, po):
                fw.op(act, lambda: A_.activation(out=yT[:, oc, 2 + c0:2 + c0 + n], in_=po[:, 0:n], func=AF.Copy), reads=[po], writes=[(yT, (oc, t))])
                fw.op(act, lambda: A_.activation(out=bfB[:, oc, c0:c0 + n], in_=po[:, 0:n], func=AF.Square), reads=[po], writes=[(bfB, (oc, t))])
            sq_proj(gi, wl, hT, ep_wo)
            post_norm(gi, l, 3, False, True)
            fw.inherit(actT, [sigA, sigB, actT] + ([KcT] if last else []))

        xv = xT_d.rearrange("(c p) n -> p c n", p=128)
        yv = yT_o.rearrange("(c p) n -> p c n", p=128)
        def main_program():
          checkpoint()
          for gi in range(NG):
              last = (gi == NG - 1)
              if gi > 0:
                  fw.dma(sp, xT[:, :, 0:G], xv[:, :, gi * G:(gi + 1) * G], sem_x, writes=[xT])
              if last:
                  fw.dma(sp, xT[:, :, G:GX], xsT_d.rearrange("(c p) n -> p c n", p=128), sem_x, writes=[xT])
              for l in range(L):
                  set_skip(gi, l, False)
                  ffn(gi, l, 0, 1, True)
                  checkpoint()
                  mixer(gi, l)
                  checkpoint()
                  set_skip(gi, l, True)
                  ffn(gi, l, 4, 5, (l < L - 1) and (gi * G >= 256))
                  set_skip(gi, l, False)
                  cur["skip"] = 0
                  checkpoint()
                  h0 = gi * G
                  if h0 < 256:
                      hn = min(256 - h0, G)
                      fw.op(dve, lambda: V_.tensor_scalar(out=xT[:, :, 0:hn], in0=xT[:, :, 0:hn], scalar1=flag[:, 0:1], scalar2=None, op0=ALU.mult),
                            reads=[xT, flag], writes=[xT])
              v0 = max(256 - gi * G, 0)
              if v0 < G:
                  fw.dma(sp, yv[:, :, gi * G + v0 - 256:(gi + 1) * G - 256], xT[:, :, v0:G], sem_y, reads=[xT])
              if last:
                  fw.dma(sp, ysT_o.rearrange("(c p) n -> p c n", p=128), xT[:, :, G:GX], sem_y, reads=[xT])
        try:
            main_program()
            assert wstate["used"] == len(plan), (wstate, len(plan))
        except _Stop:
            pass
        fw.wait_all(sp, fw.dma_sems)
        build.stats = dict(ninstr=fw.ninstr, pe=pe.sem.n, act=act.sem.n, dve=dve.sem.n, nsem=fw.nsem)
    return nc


def static_tables():
    j = np.arange(384)
    rel = 255 - j
    valid = (rel >= 0) & (rel < 128) & (j < 383)
    oh = np.zeros((32, 384), np.float32)
    bk = t5_bucket_np(np.clip(rel, 0, None))
    oh[bk[valid], j[valid]] = 1.0
    negm = np.where(valid, 0.0, NEGBIG).astype(np.float32)[None, :]
    jj = np.arange(128)
    reld = np.where(jj == 0, 0, 128 - jj)
    ohdec = np.zeros((32, 128), np.float32)
    ohdec[t5_bucket_np(reld), jj] = 1.0
    negmdec = np.zeros((1, 128), np.float32)
    return oh, negm, ohdec, negmdec


def make_in_maps(cfg, n_cores, cores_per_seq, inp):
    L, NS, SPAN, NV = cfg.L, cfg.NS, cfg.SPAN, cfg.NV
    perm = np.array([slot_head(s) for s in range(16)])
    qcols = (perm[:, None] * 64 + np.arange(64)[None, :]).reshape(-1)
    win = np.array(inp["w_in"], dtype=np.float32, copy=True)
    win[:, :, 3072:4096] = win[:, :, 3072 + qcols]
    wao = np.ascontiguousarray(np.asarray(inp["w_attn_out"], np.float32)[:, qcols, :])
    relb = np.ascontiguousarray(np.asarray(inp["rel_bias"], np.float32)[:, perm])
    relbrep = np.ascontiguousarray(np.tile(relb, (1, 8)))
    sinks = np.asarray(inp["sinks"], np.float32)[:, perm]
    sinkb = np.ascontiguousarray(np.broadcast_to(sinks.reshape(1, L * 16), (128, L * 16)))
    sinkd = np.ascontiguousarray(np.tile(sinks.T, (8, 1)))
    g = np.asarray(inp["norm_g"], np.float32)
    gT = np.ascontiguousarray(g.reshape(L, 6, 8, 128).transpose(3, 0, 1, 2).reshape(128, L * 48))
    cw = np.asarray(inp["conv_w"], np.float32)
    cwT = np.ascontiguousarray(cw.reshape(L, 3, 8, 128).transpose(3, 0, 1, 2).reshape(128, L * 24))
    oh, negm, ohdec, negmdec = static_tables()
    shared = dict(relb=relb, relbrep=relbrep, oh=oh, ohdec=ohdec, negm=negm, negmdec=negmdec, gT=gT, cwT=cwT,
                  sinkb=sinkb, sinkd=sinkd,
                  w1gu=np.ascontiguousarray(inp["w_ff1_gu"], dtype=np.float32), w1d=np.ascontiguousarray(inp["w_ff1_down"], dtype=np.float32),
                  win=win, wco=np.ascontiguousarray(inp["w_conv_out"], dtype=np.float32), wao=wao,
                  wo=np.ascontiguousarray(inp["w_out"], dtype=np.float32),
                  w2gu=np.ascontiguousarray(inp["w_ff2_gu"], dtype=np.float32), w2d=np.ascontiguousarray(inp["w_ff2_down"], dtype=np.float32))
    xp = np.asarray(inp["x_prompt"], np.float32)
    xs = np.asarray(inp["x_sample"], np.float32)
    stc = np.asarray(inp["state_conv"], np.float32)
    ck = np.asarray(inp["cache_k_win"], np.float32)
    cv = np.asarray(inp["cache_v_win"], np.float32)
    maps = []
    for core in range(n_cores):
        sq_, pos = divmod(core, cores_per_seq)
        s0 = pos * NV * 128
        xT = np.zeros((D, SPAN), np.float32)
        lo = s0 - 256
        if lo >= 0:
            xT[:, :] = xp[sq_, lo:lo + SPAN, :].T
        else:
            xT[:, 256:] = xp[sq_, 0:SPAN - 256, :].T
        b0 = core * NS
        m = dict(shared)
        m["xT"] = xT
        m["xsT"] = np.ascontiguousarray(xs[b0:b0 + NS, 0, :].T)
        m["flag"] = np.full((128, 1), 1.0 if lo >= 0 else 0.0, np.float32)
        m["stT"] = np.ascontiguousarray(stc[:, b0:b0 + NS].reshape(L, NS, 2, 8, 128).transpose(0, 4, 3, 2, 1))
        kk = ck[:, b0:b0 + NS].reshape(L, NS, 128, 2, 2, 64)
        m["kcT"] = np.ascontiguousarray(kk.transpose(0, 4, 5, 3, 1, 2).reshape(L, 128, 2, NS, 128))
        m["kc"] = np.ascontiguousarray(ck[:, b0:b0 + NS].reshape(L, NS, 128, 256))
        m["vc"] = np.ascontiguousarray(cv[:, b0:b0 + NS].reshape(L, NS, 128, 256))
        maps.append(m)
    return maps


def assemble(cfg, n_cores, cores_per_seq, res, n_seq):
    L, NS, NV = cfg.L, cfg.NS, cfg.NV
    seq = cores_per_seq * NV * 128
    yp = np.zeros((n_seq, seq, D), np.float32)
    ys = np.zeros((n_cores * NS, 1, D), np.float32)
    pc = np.zeros((L, n_seq, 2, D), np.float32)
    pk = np.zeros((L, n_seq, 128, NKV, HD), np.float32)
    pv = np.zeros((L, n_seq, 128, NKV, HD), np.float32)
    sc = np.zeros((L, n_cores * NS, 2, D), np.float32)
    sk = np.zeros((L, n_cores * NS, 128, NKV, HD), np.float32)
    sv = np.zeros((L, n_cores * NS, 128, NKV, HD), np.float32)
    for core in range(n_cores):
        r = res[core]
        sq_, pos = divmod(core, cores_per_seq)
        s0 = pos * NV * 128
        yp[sq_, s0:s0 + NV * 128, :] = np.asarray(r["yT"]).T
        b0 = core * NS
        ys[b0:b0 + NS, 0, :] = np.asarray(r["ysT"]).T
        sc[:, b0:b0 + NS] = np.asarray(r["convs"]).transpose(0, 4, 3, 2, 1).reshape(L, NS, 2, D)
        sk[:, b0:b0 + NS] = np.asarray(r["ks"]).reshape(L, NS, 128, NKV, HD)
        sv[:, b0:b0 + NS] = np.asarray(r["vs"]).reshape(L, NS, 128, NKV, HD)
        if pos == cores_per_seq - 1:
            pc[:, sq_] = np.asarray(r["convp"]).transpose(0, 3, 2, 1).reshape(L, 2, D)
            kp = np.asarray(r["kpT"]).reshape(L, 2, 64, 2, 128)
            pk[:, sq_] = kp.transpose(0, 4, 3, 1, 2).reshape(L, 128, NKV, HD)
            pv[:, sq_] = np.asarray(r["vp"]).reshape(L, 128, NKV, HD)
    return (yp, ys, pc, pk, pv, sc, sk, sv)


_CACHE = {}


def kernel(**inputs):
    cfg = Cfg()
    if "nc" not in _CACHE:
        _CACHE["nc"] = build(cfg)
    nc = _CACHE["nc"]
    in_maps = make_in_maps(cfg, 8, 4, inputs)
    res = run_bass_kernel_spmd(nc, in_maps, core_ids=list(range(8)))
    return assemble(cfg, 8, 4, res.results, 2)
```
